# Optimizing a Trainium2 kernel written in Bass

```python
import jax, jax.numpy as jnp
from jax import lax
import numpy as np

D_MODEL = 1024
BATCH = 2
SEQ = 8192
DEPTH = 1

MIX_WIDTH = D_MODEL
POOL_WIDTH = MIX_WIDTH // 2
CONV_WIDTH = MIX_WIDTH - POOL_WIDTH
POOL_WINDOWS = (2, 4, 8, 16)
N_POOL_GROUPS = len(POOL_WINDOWS)
POOL_GROUP_DIM = POOL_WIDTH // N_POOL_GROUPS
CONV_HEAD_DIM = 64
N_CONV_HEADS = CONV_WIDTH // CONV_HEAD_DIM
CONV_K = 3
IN_COLS = POOL_WIDTH + 3 * CONV_WIDTH
D_FF = ((8 * D_MODEL // 3 + 255) // 256) * 256
RMS_EPS = 1e-6

kernel_name = "hybrid_pool_shortconv_block"


def _rmsnorm(x, g):
    xf = x.astype(jnp.float32)
    inv = lax.rsqrt(jnp.mean(xf * xf, axis=-1, keepdims=True) + RMS_EPS)
    return (xf * inv).astype(x.dtype) * g


def _trailing_pool_minus_self(u, window):
    seq = u.shape[1]
    uf = u.astype(jnp.float32)
    cs = jnp.cumsum(uf, axis=1)
    cs_lag = jnp.pad(cs, ((0, 0), (window, 0), (0, 0)))[:, :seq]
    cnt = jnp.minimum(jnp.arange(1, seq + 1), window).astype(jnp.float32)
    return ((cs - cs_lag) / cnt[None, :, None] - uf).astype(u.dtype)


def _pool_mixer(v, pool_w, pool_scale):
    b, s, _ = v.shape
    vg = v.reshape(b, s, N_POOL_GROUPS, POOL_GROUP_DIM)
    pooled = jnp.stack(
        [_trailing_pool_minus_self(vg[:, :, i], w) for i, w in enumerate(POOL_WINDOWS)],
        axis=2)
    mixed = jnp.einsum('bsgc,gcd->bsgd', pooled, pool_w)
    return mixed.reshape(b, s, POOL_WIDTH) * pool_scale


def _causal_depthwise_conv(u, conv_w):
    s = u.shape[1]
    up = jnp.pad(u, ((0, 0), (CONV_K - 1, 0), (0, 0)))
    return sum(up[:, k:k + s] * conv_w[k] for k in range(CONV_K))


def _conv_mixer(gb, gc, h, conv_w):
    return gb * _causal_depthwise_conv(gc * h, conv_w)


def setup_inputs(seed: int = 0) -> dict:
    key = jax.random.key(seed)
    ks = jax.random.split(key, 12)
    f32 = jnp.float32
    x = jax.random.normal(ks[0], (BATCH, SEQ, D_MODEL), f32)
    norm1_g = 1.0 + 0.05 * jax.random.normal(ks[1], (D_MODEL,), f32)
    w_in = jax.random.normal(ks[2], (D_MODEL, IN_COLS), f32) * D_MODEL ** -0.5
    pool_w = jax.random.normal(ks[3], (N_POOL_GROUPS, POOL_GROUP_DIM, POOL_GROUP_DIM), f32) * POOL_GROUP_DIM ** -0.5
    pool_scale = 1.0 + 0.05 * jax.random.normal(ks[4], (POOL_WIDTH,), f32)
    conv_w = jax.random.normal(ks[5], (CONV_K, CONV_WIDTH), f32) * CONV_K ** -0.5
    w_out = jax.random.normal(ks[6], (MIX_WIDTH, D_MODEL), f32) * MIX_WIDTH ** -0.5
    norm2_g = 1.0 + 0.05 * jax.random.normal(ks[7], (D_MODEL,), f32)
    w_gate = jax.random.normal(ks[8], (D_MODEL, D_FF), f32) * D_MODEL ** -0.5
    w_up = jax.random.normal(ks[9], (D_MODEL, D_FF), f32) * D_MODEL ** -0.5
    w_down = jax.random.normal(ks[10], (D_FF, D_MODEL), f32) * D_FF ** -0.5
    normf_g = 1.0 + 0.05 * jax.random.normal(ks[11], (D_MODEL,), f32)
    return {"x": x, "norm1_g": norm1_g, "w_in": w_in, "pool_w": pool_w,
            "pool_scale": pool_scale, "conv_w": conv_w, "w_out": w_out,
            "norm2_g": norm2_g, "w_gate": w_gate, "w_up": w_up,
            "w_down": w_down, "normf_g": normf_g}


def reference(x, norm1_g, w_in, pool_w, pool_scale, conv_w, w_out,
              norm2_g, w_gate, w_up, w_down, normf_g):
    for _ in range(DEPTH):
        hn = _rmsnorm(x, norm1_g)
        proj = jnp.einsum('bsd,dc->bsc', hn, w_in)
        v_pool = proj[..., :POOL_WIDTH]
        gb = proj[..., POOL_WIDTH:POOL_WIDTH + CONV_WIDTH]
        gc = proj[..., POOL_WIDTH + CONV_WIDTH:POOL_WIDTH + 2 * CONV_WIDTH]
        h = proj[..., POOL_WIDTH + 2 * CONV_WIDTH:]
        y_pool = _pool_mixer(v_pool, pool_w, pool_scale)
        y_conv = _conv_mixer(gb, gc, h, conv_w)
        y_mix = jnp.concatenate([y_pool, y_conv], axis=-1)
        x = x + jnp.einsum('bsc,cd->bsd', y_mix, w_out)
        hn2 = _rmsnorm(x, norm2_g)
        g = jnp.einsum('bsd,df->bsf', hn2, w_gate)
        u = jnp.einsum('bsd,df->bsf', hn2, w_up)
        x = x + jnp.einsum('bsf,fd->bsd', jax.nn.silu(g) * u, w_down)
    return _rmsnorm(x, normf_g)
```

```python
from contextlib import ExitStack

import numpy as np
import concourse.bass as bass
import concourse.mybir as mybir
from concourse.bass_utils import run_bass_kernel_spmd

F32 = mybir.dt.float32
BF16 = mybir.dt.bfloat16
U8 = mybir.dt.uint8
ALU = mybir.AluOpType
AF = mybir.ActivationFunctionType

D = 1024
KD = 8
NT = 2048
NSUB = 16
NTILE = 4
TT = 512
H = 16
INC = 2048
DFF = 2816
EPS = 1e-6
WINS = (2, 4, 8, 16)
N_CORES = 8
FG = 4
GROUPS = [(0, 3), (3, 3), (6, 4), (10, 4), (14, 4), (18, 4)]
ARENA = 212256
GATE_JOBS = (0, 1, 4)
FINAL_SPLIT = False


class Buf:
    __slots__ = ("name", "w", "r")

    def __init__(self, name):
        self.name = name
        self.w = []
        self.r = []


def retire(*bufs):
    out = set()
    for b in bufs:
        out.update(b.w)
        out.update(b.r)
    return sorted(out)


def inherit(buf, events):
    buf.r = buf.r + list(events)


class _Rec:
    def __init__(self, sink):
        self._sink = sink

    def __getattr__(self, name):
        def f(*a, **kw):
            self._sink.append((name, a, kw))
            return None
        return f


class Op:
    __slots__ = ("idx", "eng", "calls", "deps", "est", "dma", "start", "end", "busy", "ev")


def _fsize(ap):
    n = 1
    for d in list(ap.shape)[1:]:
        n *= int(d)
    return n


def _is_psum(ap):
    try:
        return "psum" in str(ap.space).lower()
    except Exception:
        return False


class K:
    LAT = 0.25
    WINDOW = 40
    PATIENCE = 0.15

    def __init__(self, nc, es):
        self.nc = nc
        self.es = es
        self.ops = []
        self.calls = []
        self.rv = _Rec(self.calls)
        self.ra = _Rec(self.calls)
        self.rp = _Rec(self.calls)
        self.rt = _Rec(self.calls)
        self.stores = []

    def dsem(self):
        return None

    def _record(self, en, calls, reads, writes, scratch, est, dma=None):
        deps = {}
        for b in reads:
            for j in b.w:
                deps[j] = True
        for b in writes:
            for j in b.w:
                deps[j] = True
            for j in b.r:
                deps.setdefault(j, False)
        for b in scratch:
            for j in b.w + b.r:
                deps.setdefault(j, False)
        o = Op()
        o.idx = len(self.ops)
        o.eng, o.calls, o.deps, o.est, o.dma = en, calls, sorted(deps.items()), est, dma
        o.start = o.end = o.busy = o.ev = None
        self.ops.append(o)
        for b in list(writes) + list(scratch):
            b.w = [o.idx]
            b.r = []
        for b in reads:
            b.r = b.r + [o.idx]
        return o.idx

    def _cost(self, en, calls):
        t = 0.0
        for (name, a, kw) in calls:
            if en == "pe":
                if name == "transpose":
                    t += 0.075
                else:
                    rhs = kw.get("rhs", a[2] if len(a) > 2 else None)
                    n = _fsize(rhs)
                    t += 0.219 * n / 512.0 if n >= 128 else 0.035
                continue
            out = kw.get("out", a[0] if a else None)
            n = _fsize(out)
            aps = [v for v in list(a) + list(kw.values()) if hasattr(v, "shape") and hasattr(v, "space")]
            ps = any(_is_psum(v) for v in aps)
            if en == "act":
                t += 0.22 + n / 1200.0 + (0.1 if "accum_out" in kw else 0.0)
            elif en == "dve":
                if name == "tensor_copy" and ps:
                    t += 0.16 + n / 1920.0
                else:
                    t += (0.22 if name == "scalar_tensor_tensor" else 0.10) + n / 960.0 + (0.06 if ps else 0.0)
            else:
                t += 0.55 if n <= 4 else 0.75 + n / 850.0
        return t

    def op(self, en, fn, reads=(), writes=(), scratch=()):
        del self.calls[:]
        fn()
        calls = list(self.calls)
        del self.calls[:]
        return self._record(en, calls, reads, writes, scratch, self._cost(en, calls))

    def mm(self, out_ap, pairs, reads, writes):
        n = len(pairs)
        calls = [("matmul", (out_ap, l, r), {"start": i == 0, "stop": i == n - 1}) for i, (l, r) in enumerate(pairs)]
        return self._record("pe", calls, reads, writes, (), self._cost("pe", calls))

    def dma(self, qn, ds, out_ap, in_ap, reads=(), writes=(), store=False):
        nbytes = 4 * 128 * 0
        try:
            nbytes = int(in_ap.nbytes) if not store else int(out_ap.nbytes)
        except Exception:
            nbytes = 4 * int(np.prod([int(d) for d in in_ap.shape]))
        idx = self._record(qn, [(out_ap, in_ap)], reads, writes, (), 0.45 if qn == "sp" else 1.06, dma=nbytes)
        if store:
            self.stores.append(idx)
        return idx

    def schedule(self):
        ops = self.ops
        engs = ["pe", "act", "dve", "pool", "sp"]
        pend = {e: [o.idx for o in ops if o.eng == e] for e in engs}
        free = {e: 0.0 for e in engs}
        dma_free = 0.0
        left = len(ops)
        while left:
            best = None
            for e in engs:
                cands = []
                for idx in pend[e][:self.WINDOW]:
                    o = ops[idx]
                    ready = 0.0
                    ok = True
                    for (j, _) in o.deps:
                        d = ops[j]
                        if d.end is None:
                            ok = False
                            break
                        lat = 0.0 if (d.eng == e and d.dma is None) else self.LAT
                        if d.end + lat > ready:
                            ready = d.end + lat
                    if ok:
                        cands.append((max(ready, free[e]), idx))
                if not cands:
                    continue
                mn = min(c[0] for c in cands)
                st, idx = min(((c[0], c[1]) for c in cands if c[0] <= mn + self.PATIENCE), key=lambda c: c[1])
                if best is None or (st, idx) < (best[0], best[2]):
                    best = (st, e, idx)
            assert best is not None, "scheduler deadlock"
            st, e, idx = best
            o = ops[idx]
            o.start = st
            if o.dma is None:
                o.busy = o.est
                o.end = st + o.est
            else:
                o.busy = o.est
                t0 = max(st + o.est, dma_free)
                dma_free = t0 + o.dma / 340e3
                o.end = dma_free + 2.0
            free[e] = st + o.busy
            pend[e].remove(idx)
            left -= 1

    def emit(self):
        nc = self.nc
        handles = {"pe": nc.tensor, "act": nc.scalar, "dve": nc.vector, "pool": nc.gpsimd, "sp": nc.sync}
        sems = {e: self.es.enter_context(nc.semaphore("c_" + e)) for e in handles}
        cnt = {e: 0 for e in handles}
        waited = {e: {} for e in handles}
        order = sorted(self.ops, key=lambda o: (o.start, o.idx))
        nd = 0
        for o in order:
            e = o.eng
            h = handles[e]
            for (j, isw) in o.deps:
                d = self.ops[j]
                assert d.ev is not None, "dependency emitted after its consumer"
                s, v = d.ev
                if d.dma is None and d.eng == e and e == "pe":
                    continue
                if waited[e].get(s, 0) >= v:
                    continue
                h.wait_ge(s, v)
                waited[e][s] = v
            if o.dma is not None:
                nd += 1
                dsem = self.es.enter_context(nc.semaphore("d%d" % nd))
                out_ap, in_ap = o.calls[0]
                h.dma_start(out=out_ap, in_=in_ap).then_inc(dsem, 16)
                o.ev = (dsem, 16)
            else:
                ins = None
                for (name, a, kw) in o.calls:
                    ins = getattr(h, name)(*a, **kw)
                cnt[e] += 1
                ins.then_inc(sems[e], 1)
                o.ev = (sems[e], cnt[e])
        for idx in self.stores:
            s, v = self.ops[idx].ev
            nc.sync.wait_ge(s, v)


def build_program():
    nc = bass.Bass("TRN2", target_bir_lowering=False)
    dr = {}

    def din(name, shape):
        dr[name] = nc.dram_tensor(name, list(shape), F32, kind="ExternalInput").ap()
        return dr[name]

    x_d = din("x", [NT, D])
    xh_d = din("xh", [H, D])
    g1_d = din("g1", [128, D])
    g2_d = din("g2", [128, D])
    gf_d = din("gf", [128, D])
    win_d = din("w_in", [D, INC])
    pw_d = din("pool_w", [4, 128, 128])
    ps_d = din("pscale", [128, 4])
    cw_d = din("convw", [128, 12])
    wout_d = din("w_out", [D, D])
    wg_d = din("w_gate", [D, DFF])
    wu_d = din("w_up", [D, DFF])
    wd_d = din("w_down", [DFF, D])
    id_d = din("ident", [128, 128])
    ic_d = din("invcnt", [128, 64])
    y_d = nc.dram_tensor("y", [NT, D], F32, kind="ExternalOutput").ap()

    with ExitStack() as es:
        k = K(nc, es)
        rv, ra, rp, rt = k.rv, k.ra, k.rp, k.rt
        arena = nc.alloc_sbuf_tensor("arena", [128, ARENA], U8)
        base = nc.lookup_mloc(arena).addr
        off = [base]

        def region(nbytes):
            a = off[0]
            off[0] += (nbytes + 31) // 32 * 32
            return a

        cnt = [0]

        def at(addr, shape, dt, name):
            cnt[0] += 1
            return nc.alloc_sbuf_tensor_at("%s_%d" % (name, cnt[0]), list(shape), dt, offset=addr)

        X1 = region(NSUB * D * 4)
        HNT = region(KD * (NT + H) * 2)
        RA = region(32768)
        RB = region(32768)
        RC = region(29344)
        WOUT = region(16384)
        POOLW = region(1024)
        IDENT = region(256)
        CONVW = region(48)
        PSCALE = region(16)
        INVC = region(256)
        SS = region(256)
        MS = region(256)
        INV = region(256)
        MHALF = region(4)
        assert off[0] - base <= ARENA, off[0] - base

        x1 = at(X1, [128, NSUB, D], F32, "x1")
        hnT = at(HNT, [128, KD, NT + H], BF16, "hnT")
        w_in = at(RA, [128, KD, INC], BF16, "w_in")
        ymix = at(RB, [128, KD, NT], BF16, "ymix")
        w_out = at(WOUT, [128, KD, D], BF16, "w_out")
        poolw = at(POOLW, [128, 4, 128], BF16, "poolw")
        ident = at(IDENT, [128, 128], BF16, "ident")
        convw = at(CONVW, [128, 12], F32, "convw")
        pscale = at(PSCALE, [128, 4], F32, "pscale")
        invc = at(INVC, [128, 64], F32, "invc")
        ss = at(SS, [128, 64], F32, "ss")
        ms = at(MS, [128, 64], F32, "ms")
        inv = at(INV, [128, 64], F32, "inv")
        mhalf = at(MHALF, [128, 1], F32, "mhalf")
        hn = [at(RC + i * 2048, [128, D], BF16, "hn") for i in range(2)]
        gain = at(RC + 4096, [128, D], F32, "gain")
        xh = at(RC + 8192, [H, D], F32, "xh")
        V = [at(RC + 12288 + q * 2112, [128, TT + H], F32, "V") for q in range(2)]
        SA = at(RC + 16512, [128, TT + H], F32, "SA")
        SB = at(RC + 18624, [128, TT + H], F32, "SB")
        PL = [at(RC + 20736 + q * 1024, [128, TT], BF16, "PL") for q in range(2)]
        CS = at(RC + 22784, [128, TT], F32, "CS")
        U = at(RC + 24832, [128, TT + 2], F32, "U")
        TC = at(RC + 26944, [128, TT], F32, "TC")
        T16 = at(RC + 28992, [128, H], F32, "T16")
        Vc = at(RC + 29056, [128, 4, H], F32, "Vc")
        Uc = at(RC + 29312, [128, 4, 2], F32, "Uc")
        hT = [at(RC + q * 4096, [128, FG, TT], BF16, "hT") for q in range(2)]
        SG = [at(RC + 8192 + q * 2048, [128, TT], F32, "SG") for q in range(2)]
        ST = [at(RC + 12288 + q * 4096, [128, D], F32, "ST") for q in range(2)]
        ST.append(at(RC + 24576, [128, D], F32, "ST"))
        ST += [at(RB + 24576 + q * 4096, [128, D], F32, "ST") for q in range(2)]
        NST = len(ST)
        gfb = at(RC + 20480, [128, D], F32, "gfb")
        slot_base = [RA, RB]
        wg_s = [at(slot_base[q], [128, KD, TT], BF16, "wg") for q in range(2)]
        wu_s = [at(slot_base[q] + 8192, [128, KD, TT], BF16, "wu") for q in range(2)]
        wd_s = [at(slot_base[q] + 16384, [128, FG, D], BF16, "wd") for q in range(2)]

        psb = [es.enter_context(nc.psum_tensor("ps%d" % i, [128, TT], F32)) for i in range(7)]
        psT = es.enter_context(nc.psum_tensor("psT", [128, KD, 128], BF16))
        b_ps = [Buf("ps%d" % i) for i in range(7)]
        b_psT = Buf("psT")
        psb.append(psT.bitcast(F32).reshape([128, TT]))
        b_ps.append(b_psT)

        b_x = [Buf("x%d" % s) for s in range(NSUB)]
        b_xh = Buf("xh")
        b_hnT = [Buf("hnT%d" % s) for s in range(NSUB)]
        b_hnTh = Buf("hnTh")
        b_hn = [Buf("hn0"), Buf("hn1")]
        b_gain = Buf("gain")
        b_winP = Buf("winP")
        b_winB = [Buf("winB0"), Buf("winB1")]
        b_winC = [Buf("winC0"), Buf("winC1")]
        b_winH = [Buf("winH0"), Buf("winH1")]
        b_win_all = [b_winP] + b_winB + b_winC + b_winH
        b_wout = [Buf("wout0"), Buf("wout1")]
        b_convw, b_pscale, b_invc = Buf("convw"), Buf("pscale"), Buf("invc")
        b_ident = Buf("ident")
        b_poolw = Buf("poolw")
        b_mhalf = Buf("mhalf")
        b_ymix = [[Buf("ym%d_%d" % (t, c)) for c in range(KD)] for t in range(NTILE)]

        cs = k.dsem()
        gs = k.dsem()
        xs_sem = k.dsem()
        x_sems = [k.dsem() for _ in range(NSUB)]
        k.dma("sp", xs_sem, xh[:], xh_d[:], writes=[b_xh])
        k.dma("sp", x_sems[0], x1[:, 0, :], x_d[0:128, :], writes=[b_x[0]])
        k.dma("sp", gs, gain[:], g1_d[:], writes=[b_gain])
        for s in range(1, 4):
            k.dma("sp", x_sems[s], x1[:, s, :], x_d[s * 128:(s + 1) * 128, :], writes=[b_x[s]])
        k.dma("sp", cs, convw[:], cw_d[:], writes=[b_convw])
        k.dma("sp", cs, pscale[:], ps_d[:], writes=[b_pscale])
        k.dma("sp", cs, invc[:], ic_d[:], writes=[b_invc])
        k.op("dve", lambda: rv.memset(mhalf[:], -0.5), writes=[b_mhalf])
        b_warm = Buf("warm")
        k.op("act", lambda: ra.activation(out=ss[:, 63:64], in_=mhalf[:, 0:1], func=AF.Tanh),
             reads=[b_mhalf], writes=[b_warm])

        ids = k.dsem()
        k.dma("pool", ids, ident[:], id_d[:], writes=[b_ident])
        win_v = win_d.rearrange("(k p) c -> p k c", p=128)

        def load_mixer_weights(gate):
            k.dma("pool", k.dsem(), w_in[:, :, 0:512], win_v[:, :, 0:512], reads=[gate[0]], writes=[b_winP])
            k.dma("pool", k.dsem(), poolw[:], pw_d.rearrange("g c d -> c g d"), reads=[gate[0]], writes=[b_poolw])
            for hf in range(2):
                for (c0, bb) in ((1024, b_winC), (1536, b_winH), (512, b_winB)):
                    a = c0 + hf * 256
                    k.dma("pool", k.dsem(), w_in[:, :, a:a + 256], win_v[:, :, a:a + 256],
                          reads=[gate[1 + hf]], writes=[bb[hf]])
            for s in range(4, NSUB):
                k.dma("sp", x_sems[s], x1[:, s, :], x_d[s * 128:(s + 1) * 128, :],
                      reads=b_win_all, writes=[b_x[s]])
        wos = k.dsem()
        wout_v = wout_d.rearrange("(k p) d -> p k d", p=128)

        def load_wout():
            for hh in range(2):
                k.dma("pool", wos, w_out[:, :, hh * 512:(hh + 1) * 512], wout_v[:, :, hh * 512:(hh + 1) * 512],
                      reads=[b_x[NSUB - 1]], writes=[b_wout[hh]])

        stat_i = [0]

        class NormJob:
            pass

        def nj_new(xsrc_ap, rows, b_xsrc, gain_ap, b_g, out_ap, b_out, junk_ap, b_jk):
            j = NormJob()
            j.x, j.rows, j.bx, j.g, j.bg = xsrc_ap, rows, b_xsrc, gain_ap, b_g
            j.out, j.bout, j.junk, j.bjk = out_ap, b_out, junk_ap, b_jk
            j.i = stat_i[0]
            stat_i[0] += 1
            j.bs, j.bm, j.bi = Buf("ss"), Buf("ms"), Buf("inv")
            return j

        def nj_square(j):
            i, rows = j.i, j.rows
            k.op("act", lambda: ra.activation(out=j.junk, in_=j.x, func=AF.Square,
                                                     accum_out=ss[:rows, i:i + 1]),
                 reads=[j.bx], writes=[j.bs], scratch=[j.bjk])

        def nj_inv(j):
            i, rows = j.i, j.rows
            k.op("dve", lambda: rv.tensor_scalar(ms[:rows, i:i + 1], ss[:rows, i:i + 1], 1.0 / D, EPS,
                                                        op0=ALU.mult, op1=ALU.add),
                 reads=[j.bs], writes=[j.bm])
            k.op("pool", lambda: rp.tensor_tensor(out=inv[:rows, i:i + 1], in0=ms[:rows, i:i + 1],
                                                         in1=mhalf[:rows, 0:1], op=ALU.pow),
                 reads=[j.bm, b_mhalf], writes=[j.bi])

        def nj_scale(j, split=False):
            i, rows = j.i, j.rows
            if split:
                k.op("act", lambda: ra.mul(out=j.out, in_=j.x, mul=inv[:rows, i:i + 1]),
                     reads=[j.bx, j.bi], writes=[j.bout])
                k.op("pool", lambda: rp.tensor_tensor(out=j.out, in0=j.out, in1=j.g, op=ALU.mult),
                     reads=[j.bout, j.bg], writes=[j.bout])
                return
            k.op("dve", lambda: rv.scalar_tensor_tensor(out=j.out, in0=j.x, scalar=inv[:rows, i:i + 1],
                                                               in1=j.g, op0=ALU.mult, op1=ALU.mult),
                 reads=[j.bx, j.bi, j.bg], writes=[j.bout])

        def norm_stage_b(rows, slot, col0, b_dst, copy_eng="act"):
            def tr():
                ins = None
                for kk in range(KD):
                    ins = rt.transpose(out=psT[:, kk, :rows], in_=hn[slot][:rows, kk * 128:(kk + 1) * 128],
                                              identity=ident[:rows, :rows])
                return ins
            k.op("pe", tr, reads=[b_hn[slot], b_ident], writes=[b_psT])
            dst = hnT[:, :, col0:col0 + rows]
            if copy_eng == "dve":
                k.op("dve", lambda: rv.tensor_copy(dst, psT[:, :, :rows]), reads=[b_psT], writes=[b_dst])
            else:
                k.op("act", lambda: ra.copy(out=dst, in_=psT[:, :, :rows]), reads=[b_psT], writes=[b_dst])

        p1_items = [("h", H)] + [(s, 128) for s in range(NSUB)]

        p1_jobs = {}
        for q in range(5):
            hn.append(at(RB + q * 2048, [128, D], BF16, "hns"))
            b_hn.append(Buf("hns%d" % q))

        def p1_slot(idx):
            return 2 + idx if idx < 5 else idx % 2

        def p1_job(idx):
            if idx not in p1_jobs:
                s_, rows = p1_items[idx]
                sl = p1_slot(idx)
                if s_ == "h":
                    p1_jobs[idx] = nj_new(xh[:rows, :], rows, b_xh, gain[:rows, :], b_gain,
                                          hn[sl][:rows, :], b_hn[sl], hn[sl][:rows, :], b_hn[sl])
                else:
                    p1_jobs[idx] = nj_new(x1[:, s_, :], rows, b_x[s_], gain[:, :], b_gain,
                                          hn[sl][:, :], b_hn[sl], hn[sl][:, :], b_hn[sl])
            return p1_jobs[idx]

        def p1_sq(idx):
            nj_square(p1_job(idx))
            nj_inv(p1_job(idx))

        def p1_sc(idx):
            nj_scale(p1_job(idx))

        def p1_b(idx):
            s_, rows = p1_items[idx]
            ce = "dve" if idx in (2, 4) else "act"
            if s_ == "h":
                norm_stage_b(rows, p1_slot(idx), 0, b_hnTh, ce)
            else:
                norm_stage_b(rows, p1_slot(idx), H + s_ * 128, b_hnT[s_], ce)

        b_V = [Buf("V0"), Buf("V1")]
        b_SA, b_SB = Buf("SA"), Buf("SB")
        b_PL = [Buf("PL0"), Buf("PL1")]
        b_CS, b_TC, b_T16, b_U = Buf("CS"), Buf("TC"), Buf("T16"), Buf("U")
        b_Vc = [Buf("Vc%d" % i) for i in range(4)]
        b_Uc = [Buf("Uc%d" % i) for i in range(4)]
        p1_bufs = b_hn[0:2] + [b_gain, b_xh]
        p2_bufs = b_V + [b_SA, b_SB] + b_PL + [b_CS, b_TC, b_T16, b_U] + b_Vc + b_Uc

        rot = {}

        def next_bank(role_banks, key):
            i = rot.get(key, 0)
            rot[key] = i + 1
            return role_banks[i % len(role_banks)]

        def hn_reads(t):
            return [b_hnT[4 * t + j] for j in range(4)]

        def inproj_pairs(col0, ncols, tok0, ntok):
            return [(w_in[:, kk, col0:col0 + ncols], hnT[:, kk, tok0:tok0 + ntok]) for kk in range(KD)]

        M_banks = [0, 1, 2, 3, 4, 5, 6]
        pcount = [0]

        def pool_front(gi, t):
            q = pcount[0] % 2
            pcount[0] += 1
            w = WINS[gi]
            c0 = gi * 128
            if t == 0:
                bk = next_bank(M_banks, "m")
                k.mm(psb[bk][:, 0:H], inproj_pairs(c0, 128, 0, H), reads=[b_winP, b_hnTh], writes=[b_ps[bk]])
                k.op("act", lambda: ra.copy(out=V[q][:, 0:H], in_=psb[bk][:, 0:H]),
                     reads=[b_ps[bk]], writes=[b_V[q]])
            else:
                k.op("act", lambda: ra.copy(out=V[q][:, 0:H], in_=Vc[:, gi, :]),
                     reads=[b_Vc[gi]], writes=[b_V[q]])
            bk = next_bank(M_banks, "m")
            k.mm(psb[bk][:, :], inproj_pairs(c0, 128, H + t * TT, TT), reads=[b_winP] + hn_reads(t),
                 writes=[b_ps[bk]])
            k.op("act", lambda: ra.copy(out=V[q][:, H:H + TT], in_=psb[bk][:, :]),
                 reads=[b_ps[bk]], writes=[b_V[q]])
            if t < NTILE - 1:
                k.op("act", lambda: ra.copy(out=Vc[:, gi, :], in_=V[q][:, TT:TT + H]),
                     reads=[b_V[q]], writes=[b_Vc[gi]])
            L = TT + H
            k.op("pool", lambda: rp.tensor_tensor(out=SA[:, 1:L], in0=V[q][:, 1:L], in1=V[q][:, 0:L - 1],
                                                         op=ALU.add), reads=[b_V[q]], writes=[b_SA])
            S, bS = SA, b_SA
            if w >= 4:
                k.op("pool", lambda: rp.tensor_tensor(out=SB[:, 3:L], in0=SA[:, 3:L], in1=SA[:, 1:L - 2],
                                                             op=ALU.add), reads=[b_SA], writes=[b_SB])
                S, bS = SB, b_SB
            if w >= 8:
                k.op("pool", lambda: rp.tensor_tensor(out=SA[:, 7:L], in0=SB[:, 7:L], in1=SB[:, 3:L - 4],
                                                             op=ALU.add), reads=[b_SB], writes=[b_SA])
                S, bS = SA, b_SA
            if w >= 16:
                k.op("pool", lambda: rp.tensor_tensor(out=SB[:, 15:L], in0=SA[:, 15:L], in1=SA[:, 7:L - 8],
                                                             op=ALU.add), reads=[b_SA], writes=[b_SB])
                S, bS = SB, b_SB
            k.op("dve", lambda: rv.scalar_tensor_tensor(out=PL[q][:, :], in0=S[:, H:L], scalar=1.0 / w,
                                                               in1=V[q][:, H:L], op0=ALU.mult, op1=ALU.subtract),
                 reads=[bS, b_V[q]], writes=[b_PL[q]])
            if t == 0:
                k.op("dve", lambda: rv.tensor_tensor(out=T16[:, :], in0=S[:, H:2 * H],
                                                            in1=invc[:, gi * H:(gi + 1) * H], op=ALU.mult),
                     reads=[bS, b_invc], writes=[b_T16])
                k.op("dve", lambda: rv.tensor_tensor(out=PL[q][:, 0:H], in0=T16[:, :], in1=V[q][:, H:2 * H],
                                                            op=ALU.subtract),
                     reads=[b_T16, b_V[q], b_PL[q]], writes=[b_PL[q]])

            def back():
                bk2 = next_bank(M_banks, "m")
                k.mm(psb[bk2][:, :], [(poolw[:, gi, :], PL[q][:, :])], reads=[b_poolw, b_PL[q]], writes=[b_ps[bk2]])
                k.op("act", lambda: ra.mul(out=ymix[:, gi, t * TT:(t + 1) * TT], in_=psb[bk2][:, :],
                                                  mul=pscale[:, gi:gi + 1]),
                     reads=[b_ps[bk2], b_pscale], writes=[b_ymix[t][gi]])
            return back

        def conv_front(j, t):
            hf = j // 2
            cB, cC, ch = 512 + j * 128, 1024 + j * 128, 1536 + j * 128
            if t == 0:
                bkc = next_bank(M_banks, "m")
                k.mm(psb[bkc][:, 0:2], inproj_pairs(cC, 128, H - 2, 2), reads=[b_winC[hf], b_hnTh],
                     writes=[b_ps[bkc]])
                bkh = next_bank(M_banks, "m")
                k.mm(psb[bkh][:, 0:2], inproj_pairs(ch, 128, H - 2, 2), reads=[b_winH[hf], b_hnTh],
                     writes=[b_ps[bkh]])
                k.op("act", lambda: ra.copy(out=CS[:, 0:2], in_=psb[bkc][:, 0:2]),
                     reads=[b_ps[bkc]], writes=[b_CS])
                k.op("dve", lambda: rv.tensor_tensor(out=U[:, 0:2], in0=CS[:, 0:2], in1=psb[bkh][:, 0:2],
                                                            op=ALU.mult),
                     reads=[b_CS, b_ps[bkh]], writes=[b_U])
            else:
                k.op("act", lambda: ra.copy(out=U[:, 0:2], in_=Uc[:, j, :]),
                     reads=[b_Uc[j]], writes=[b_U])
            bkc = next_bank(M_banks, "m")
            k.mm(psb[bkc][:, :], inproj_pairs(cC, 128, H + t * TT, TT), reads=[b_winC[hf]] + hn_reads(t),
                 writes=[b_ps[bkc]])
            bkh = next_bank(M_banks, "m")
            k.mm(psb[bkh][:, :], inproj_pairs(ch, 128, H + t * TT, TT), reads=[b_winH[hf]] + hn_reads(t),
                 writes=[b_ps[bkh]])
            bkb = next_bank(M_banks, "m")
            k.mm(psb[bkb][:, :], inproj_pairs(cB, 128, H + t * TT, TT), reads=[b_winB[hf]] + hn_reads(t),
                 writes=[b_ps[bkb]])
            k.op("act", lambda: ra.copy(out=CS[:, :], in_=psb[bkc][:, :]), reads=[b_ps[bkc]], writes=[b_CS])
            k.op("dve", lambda: rv.tensor_tensor(out=U[:, 2:TT + 2], in0=CS[:, :], in1=psb[bkh][:, :],
                                                        op=ALU.mult),
                 reads=[b_CS, b_ps[bkh], b_U], writes=[b_U])
            k.op("act", lambda: ra.mul(out=TC[:, :], in_=U[:, 2:TT + 2], mul=convw[:, j * 3 + 2:j * 3 + 3]),
                 reads=[b_U, b_convw], writes=[b_TC])
            if t < NTILE - 1:
                k.op("act", lambda: ra.copy(out=Uc[:, j, :], in_=U[:, TT:TT + 2]),
                     reads=[b_U], writes=[b_Uc[j]])
            k.op("dve", lambda: rv.scalar_tensor_tensor(out=TC[:, :], in0=U[:, 1:TT + 1],
                                                               scalar=convw[:, j * 3 + 1:j * 3 + 2], in1=TC[:, :],
                                                               op0=ALU.mult, op1=ALU.add),
                 reads=[b_U, b_convw, b_TC], writes=[b_TC])
            k.op("dve", lambda: rv.scalar_tensor_tensor(out=TC[:, :], in0=U[:, 0:TT],
                                                               scalar=convw[:, j * 3:j * 3 + 1], in1=TC[:, :],
                                                               op0=ALU.mult, op1=ALU.add),
                 reads=[b_U, b_convw, b_TC], writes=[b_TC])
            k.op("dve", lambda: rv.tensor_tensor(out=ymix[:, 4 + j, t * TT:(t + 1) * TT], in0=TC[:, :],
                                                        in1=psb[bkb][:, :], op=ALU.mult),
                 reads=[b_TC, b_ps[bkb]], writes=[b_ymix[t][4 + j]])
            return None

        for idx in range(5):
            p1_sq(idx)
        load_mixer_weights([p1_job(GATE_JOBS[0]).bm, p1_job(GATE_JOBS[1]).bm, p1_job(GATE_JOBS[2]).bm])
        for idx in range(5):
            p1_sc(idx)
        for idx in range(5):
            p1_b(idx)
        ev_hns = retire(*b_hn[2:7])
        for tt_ in range(NTILE):
            for cc in range(3):
                inherit(b_ymix[tt_][cc], ev_hns)
        for t in range(NTILE):
            if t == 0:
                order = [("p", 0), ("p", 1), ("p", 2), ("p", 3), ("c", 0), ("c", 1), ("c", 2), ("c", 3)]
            else:
                order = [("p", 0), ("c", 0), ("p", 1), ("c", 1), ("p", 2), ("c", 2), ("p", 3), ("c", 3)]
            pending = []
            nb = 1 + 4 * (t + 1)
            if t + 1 < NTILE:
                p1_sq(nb)
            for i, (kind, a) in enumerate(order):
                back = pool_front(a, t) if kind == "p" else conv_front(a, t)
                if kind == "p" and pending:
                    pending.pop(0)()
                if back is not None:
                    pending.append(back)
                if t + 1 < NTILE:
                    kq = i // 2
                    if i % 2 == 0:
                        p1_sc(nb + kq)
                        if i == 0:
                            p1_sq(nb + 1)
                    else:
                        p1_b(nb + kq)
                        if 1 <= kq <= 2:
                            p1_sq(nb + kq + 1)
            while pending:
                pending.pop(0)()
            if t == 0:
                load_wout()

        ffn_sems = [k.dsem(), k.dsem()]
        b_wg = [Buf("wg0"), Buf("wg1")]
        b_wu = [Buf("wu0"), Buf("wu1")]
        b_wd = [Buf("wd0"), Buf("wd1")]
        wg_v = wg_d.rearrange("(k p) f -> p k f", p=128)
        wu_v = wu_d.rearrange("(k p) f -> p k f", p=128)
        wd_v = wd_d.rearrange("(c p) d -> p c d", p=128)

        def load_group(g):
            q = g % 2
            c0, ncn = GROUPS[g]
            f0, nf = c0 * 128, ncn * 128
            ds = ffn_sems[q]
            k.dma("pool", ds, wg_s[q][:, :, 0:nf], wg_v[:, :, f0:f0 + nf], writes=[b_wg[q]])
            k.dma("pool", ds, wu_s[q][:, :, 0:nf], wu_v[:, :, f0:f0 + nf], writes=[b_wu[q]])
            k.dma("pool", ds, wd_s[q][:, 0:ncn, :], wd_v[:, c0:c0 + ncn, :], writes=[b_wd[q]])

        ev_ra = retire(*b_win_all)
        for b in (b_wg[0], b_wu[0], b_wd[0]):
            inherit(b, ev_ra)
        load_group(0)

        k.dma("sp", gs, gain[:], g2_d[:], writes=[b_gain])
        O_banks = [0, 1, 2, 3, 4, 5]
        rot["o"] = 0

        def p3_front(s, halves):
            t, s4 = s // 4, s % 4
            for hh in halves:
                bk = next_bank(O_banks, "o")
                pairs = [(ymix[:, kk, s * 128:(s + 1) * 128], w_out[:, kk, hh * 512:(hh + 1) * 512])
                         for kk in range(KD)]
                k.mm(psb[bk][:, :], pairs, reads=[b_wout[hh]] + b_ymix[t], writes=[b_ps[bk]])
                k.op("dve", lambda: rv.tensor_tensor(out=x1[:, s, hh * 512:(hh + 1) * 512],
                                                            in0=x1[:, s, hh * 512:(hh + 1) * 512],
                                                            in1=psb[bk][:, :], op=ALU.add),
                     reads=[b_x[s], b_ps[bk]], writes=[b_x[s]])

        junk3 = at(RA + 24576, [128, D], BF16, "junk3")
        b_junk3 = Buf("junk3")
        inherit(b_junk3, ev_ra)
        for q in range(2):
            hn.append(at(RA + 26624 + q * 2048, [128, D], BF16, "hnx"))
            bq = Buf("hn%d" % (2 + q))
            inherit(bq, ev_ra)
            b_hn.append(bq)
        p3_slots = [0, 1, len(hn) - 2, len(hn) - 1]
        p3_jobs = [nj_new(x1[:, s, :], 128, b_x[s], gain[:, :], b_gain, hn[p3_slots[s % 4]][:, :],
                          b_hn[p3_slots[s % 4]], junk3[:, :], b_junk3) for s in range(NSUB)]
        for i in range(NSUB + 3):
            if i < NSUB:
                p3_front(i, (0,))
                p3_front(i, (1,))
                nj_square(p3_jobs[i])
            if 0 <= i - 1 < NSUB:
                nj_inv(p3_jobs[i - 1])
            if 0 <= i - 2 < NSUB:
                nj_scale(p3_jobs[i - 2])
            if 0 <= i - 3 < NSUB:
                norm_stage_b(128, p3_slots[(i - 3) % 4], H + (i - 3) * 128, b_hnT[i - 3])

        ev_rb = retire(*[b for row in b_ymix for b in row])
        for b in (b_wg[1], b_wu[1], b_wd[1]):
            inherit(b, ev_rb)
        load_group(1)

        ev_p3 = retire(*(p1_bufs + p2_bufs))
        b_hT = [Buf("hT0"), Buf("hT1")]
        b_SG = [Buf("SG0"), Buf("SG1")]
        b_ST = [Buf("ST%d" % q) for q in range(NST)]
        b_gf = Buf("gf")
        for b in b_hT + b_SG + b_ST + [b_gf]:
            inherit(b, ev_p3)
        for b in b_ST[3:]:
            inherit(b, ev_rb)
        gfs = k.dsem()
        k.dma("sp", gfs, gfb[:], gf_d[:], writes=[b_gf])
        st_sems = [k.dsem(), k.dsem()]

        G_banks = [0, 1]
        U_banks = [2, 3]
        D_banks = [4, 5, 6, 7]
        ffn_items = [(g, t) for g in range(len(GROUPS)) for t in range(NTILE)]

        def ffn_gu(n, c):
            g, t = ffn_items[n]
            q, hq = g % 2, n % 2
            bg = next_bank(G_banks, "g")
            bu = next_bank(U_banks, "u")
            tok0 = H + t * TT
            k.mm(psb[bg][:, :], [(wg_s[q][:, kk, c * 128:(c + 1) * 128], hnT[:, kk, tok0:tok0 + TT]) for kk in range(KD)],
                 reads=[b_wg[q]] + hn_reads(t), writes=[b_ps[bg]])
            k.mm(psb[bu][:, :], [(wu_s[q][:, kk, c * 128:(c + 1) * 128], hnT[:, kk, tok0:tok0 + TT]) for kk in range(KD)],
                 reads=[b_wu[q]] + hn_reads(t), writes=[b_ps[bu]])
            sq = next_bank([0, 1], "sg")
            k.op("act", lambda: ra.activation(out=SG[sq][:, :], in_=psb[bg][:, :], func=AF.Tanh, scale=0.5),
                 reads=[b_ps[bg]], writes=[b_SG[sq]])
            k.op("dve", lambda: rv.scalar_tensor_tensor(out=SG[sq][:, :], in0=SG[sq][:, :], scalar=1.0,
                                                        in1=psb[bg][:, :], op0=ALU.add, op1=ALU.mult),
                 reads=[b_SG[sq], b_ps[bg]], writes=[b_SG[sq]])
            k.op("dve", lambda: rv.scalar_tensor_tensor(out=hT[hq][:, c, :], in0=SG[sq][:, :], scalar=0.5,
                                                        in1=psb[bu][:, :], op0=ALU.mult, op1=ALU.mult),
                 reads=[b_SG[sq], b_ps[bu], b_hT[hq]], writes=[b_hT[hq]])

        def ffn_down(n, dsel):
            g, t = ffn_items[n]
            q, hq = g % 2, n % 2
            ncn = GROUPS[g][1]
            for s4 in (dsel // 2,):
                s = 4 * t + s4
                for hh in (dsel % 2,):
                    bk = next_bank(D_banks, "d")
                    pairs = [(hT[hq][:, c, s4 * 128:(s4 + 1) * 128], wd_s[q][:, c, hh * 512:(hh + 1) * 512])
                             for c in range(ncn)]
                    k.mm(psb[bk][:, :], pairs, reads=[b_hT[hq], b_wd[q]], writes=[b_ps[bk]])
                    k.op("dve", lambda: rv.tensor_tensor(out=x1[:, s, hh * 512:(hh + 1) * 512],
                                                                in0=x1[:, s, hh * 512:(hh + 1) * 512],
                                                                in1=psb[bk][:, :], op=ALU.add),
                         reads=[b_x[s], b_ps[bk]], writes=[b_x[s]])
                if g == len(GROUPS) - 1 and dsel % 2 == 1:
                    final_step(s)

        fin_jobs = [nj_new(x1[:, s, :], 128, b_x[s], gfb[:, :], b_gf, ST[s % NST][:, :], b_ST[s % NST],
                           junk3[:, :], b_junk3) for s in range(NSUB)]

        def final_step(i):
            if i < NSUB:
                nj_square(fin_jobs[i])
            if 0 <= i - 1 < NSUB:
                nj_inv(fin_jobs[i - 1])
            if 0 <= i - 2 < NSUB:
                s2 = i - 2
                nj_scale(fin_jobs[s2], split=FINAL_SPLIT)
                k.dma("sp", None, y_d[s2 * 128:(s2 + 1) * 128, :], ST[s2 % NST][:, :], reads=[b_ST[s2 % NST]],
                      store=True)

        nitems = len(ffn_items)
        for c in range(GROUPS[0][1]):
            ffn_gu(0, c)
        for n in range(nitems):
            g, t = ffn_items[n]
            nxt = n + 1
            gn = GROUPS[ffn_items[nxt][0]][1] if nxt < nitems else 0
            if gn:
                ffn_gu(nxt, 0)
            for dsel in range(8):
                ffn_down(n, dsel)
                c = (dsel + 1) // 2
                if dsel % 2 == 1 and 1 <= c < gn:
                    ffn_gu(nxt, c)
            if t == NTILE - 1 and g + 2 < len(GROUPS):
                load_group(g + 2)

        final_step(NSUB)
        final_step(NSUB + 1)
        k.schedule()
        k.emit()
    return nc


def _prep_inputs(inputs):
    f = lambda a: np.ascontiguousarray(np.asarray(a, dtype=np.float32))
    x = f(inputs["x"])
    rep = lambda g: np.ascontiguousarray(np.broadcast_to(f(g)[None, :], (128, D)))
    shared = {
        "g1": rep(inputs["norm1_g"]),
        "g2": rep(inputs["norm2_g"]),
        "gf": rep(inputs["normf_g"]),
        "w_in": f(inputs["w_in"]),
        "pool_w": f(inputs["pool_w"]),
        "pscale": np.ascontiguousarray(f(inputs["pool_scale"]).reshape(4, 128).T),
        "convw": np.ascontiguousarray(f(inputs["conv_w"]).reshape(3, 4, 128).transpose(2, 1, 0).reshape(128, 12)),
        "w_out": f(inputs["w_out"]),
        "w_gate": f(inputs["w_gate"]),
        "w_up": f(inputs["w_up"]),
        "w_down": f(inputs["w_down"]),
        "ident": np.eye(128, dtype=np.float32),
    }
    ic_start = np.zeros((4, H), np.float32)
    ic_mid = np.zeros((4, H), np.float32)
    for gi, w in enumerate(WINS):
        for t in range(H):
            ic_start[gi, t] = 1.0 / min(t + 1, w)
            ic_mid[gi, t] = 1.0 / w
    in_maps = []
    for c in range(N_CORES):
        b, qq = divmod(c, 4)
        t0 = qq * NT
        m = dict(shared)
        m["x"] = np.ascontiguousarray(x[b, t0:t0 + NT, :])
        if qq == 0:
            m["xh"] = np.zeros((H, D), np.float32)
            ic = ic_start
        else:
            m["xh"] = np.ascontiguousarray(x[b, t0 - H:t0, :])
            ic = ic_mid
        m["invcnt"] = np.ascontiguousarray(np.broadcast_to(ic.reshape(1, 4 * H), (128, 4 * H)))
        in_maps.append(m)
    return in_maps


def kernel(**inputs):
    in_maps = _prep_inputs(inputs)
    nc = build_program()
    res = run_bass_kernel_spmd(nc, in_maps, core_ids=list(range(N_CORES)))
    out = np.empty((2, 4 * NT, D), np.float32)
    for c in range(N_CORES):
        b, qq = divmod(c, 4)
        out[b, qq * NT:(qq + 1) * NT, :] = np.asarray(res.results[c]["y"], dtype=np.float32)
    return out
```

```python
from contextlib import ExitStack

import numpy as np
import concourse.bass as bass
import concourse.mybir as mybir
from concourse.bass_utils import run_bass_kernel_spmd

F32 = mybir.dt.float32
BF16 = mybir.dt.bfloat16
U8 = mybir.dt.uint8
ALU = mybir.AluOpType
AF = mybir.ActivationFunctionType

D = 1024
KD = 8
NT = 2048
NSUB = 16
NTILE = 4
TT = 512
H = 16
INC = 2048
DFF = 2816
EPS = 1e-6
WINS = (2, 4, 8, 16)
N_CORES = 8
FG = 4
GROUPS = [(0, 3), (3, 3), (6, 4), (10, 4), (14, 4), (18, 4)]
ARENA = 212256
GATE_JOBS = (0, 1, 2)
FINAL_SPLIT = False


class Buf:
    __slots__ = ("name", "w", "r")

    def __init__(self, name):
        self.name = name
        self.w = []
        self.r = []


def retire(*bufs):
    out = set()
    for b in bufs:
        out.update(b.w)
        out.update(b.r)
    return sorted(out)


def inherit(buf, events):
    buf.r = buf.r + list(events)


class _Rec:
    def __init__(self, sink):
        self._sink = sink

    def __getattr__(self, name):
        def f(*a, **kw):
            self._sink.append((name, a, kw))
            return None
        return f


class Op:
    __slots__ = ("idx", "eng", "calls", "deps", "est", "dma", "start", "end", "busy", "ev")


def _fsize(ap):
    n = 1
    for d in list(ap.shape)[1:]:
        n *= int(d)
    return n


def _is_psum(ap):
    try:
        return "psum" in str(ap.space).lower()
    except Exception:
        return False


class K:
    LAT = 0.25
    WINDOW = 40
    PATIENCE = 0.15

    def __init__(self, nc, es):
        self.nc = nc
        self.es = es
        self.ops = []
        self.calls = []
        self.rv = _Rec(self.calls)
        self.ra = _Rec(self.calls)
        self.rp = _Rec(self.calls)
        self.rt = _Rec(self.calls)
        self.stores = []

    def dsem(self):
        return None

    def _record(self, en, calls, reads, writes, scratch, est, dma=None):
        deps = {}
        for b in reads:
            for j in b.w:
                deps[j] = True
        for b in writes:
            for j in b.w:
                deps[j] = True
            for j in b.r:
                deps.setdefault(j, False)
        for b in scratch:
            for j in b.w + b.r:
                deps.setdefault(j, False)
        o = Op()
        o.idx = len(self.ops)
        o.eng, o.calls, o.deps, o.est, o.dma = en, calls, sorted(deps.items()), est, dma
        o.start = o.end = o.busy = o.ev = None
        self.ops.append(o)
        for b in list(writes) + list(scratch):
            b.w = [o.idx]
            b.r = []
        for b in reads:
            b.r = b.r + [o.idx]
        return o.idx

    def _cost(self, en, calls):
        t = 0.0
        for (name, a, kw) in calls:
            if en == "pe":
                if name == "transpose":
                    t += 0.075
                else:
                    rhs = kw.get("rhs", a[2] if len(a) > 2 else None)
                    n = _fsize(rhs)
                    t += 0.219 * n / 512.0 if n >= 128 else 0.035
                continue
            out = kw.get("out", a[0] if a else None)
            n = _fsize(out)
            aps = [v for v in list(a) + list(kw.values()) if hasattr(v, "shape") and hasattr(v, "space")]
            ps = any(_is_psum(v) for v in aps)
            if en == "act":
                t += 0.22 + n / 1200.0 + (0.1 if "accum_out" in kw else 0.0)
            elif en == "dve":
                if name == "tensor_copy" and ps:
                    t += 0.16 + n / 1920.0
                else:
                    t += (0.22 if name == "scalar_tensor_tensor" else 0.10) + n / 960.0 + (0.06 if ps else 0.0)
            else:
                t += 0.55 if n <= 4 else 0.75 + n / 850.0
        return t

    def op(self, en, fn, reads=(), writes=(), scratch=()):
        del self.calls[:]
        fn()
        calls = list(self.calls)
        del self.calls[:]
        return self._record(en, calls, reads, writes, scratch, self._cost(en, calls))

    def mm(self, out_ap, pairs, reads, writes):
        n = len(pairs)
        calls = [("matmul", (out_ap, l, r), {"start": i == 0, "stop": i == n - 1}) for i, (l, r) in enumerate(pairs)]
        return self._record("pe", calls, reads, writes, (), self._cost("pe", calls))

    def dma(self, qn, ds, out_ap, in_ap, reads=(), writes=(), store=False):
        nbytes = 4 * 128 * 0
        try:
            nbytes = int(in_ap.nbytes) if not store else int(out_ap.nbytes)
        except Exception:
            nbytes = 4 * int(np.prod([int(d) for d in in_ap.shape]))
        idx = self._record(qn, [(out_ap, in_ap)], reads, writes, (), 0.45 if qn == "sp" else 1.06, dma=nbytes)
        if store:
            self.stores.append(idx)
        return idx

    def schedule(self):
        ops = self.ops
        engs = ["pe", "act", "dve", "pool", "sp"]
        pend = {e: [o.idx for o in ops if o.eng == e] for e in engs}
        free = {e: 0.0 for e in engs}
        dma_free = 0.0
        left = len(ops)
        while left:
            best = None
            for e in engs:
                cands = []
                for idx in pend[e][:self.WINDOW]:
                    o = ops[idx]
                    ready = 0.0
                    ok = True
                    for (j, _) in o.deps:
                        d = ops[j]
                        if d.end is None:
                            ok = False
                            break
                        lat = 0.0 if (d.eng == e and d.dma is None) else self.LAT
                        if d.end + lat > ready:
                            ready = d.end + lat
                    if ok:
                        cands.append((max(ready, free[e]), idx))
                if not cands:
                    continue
                mn = min(c[0] for c in cands)
                st, idx = min(((c[0], c[1]) for c in cands if c[0] <= mn + self.PATIENCE), key=lambda c: c[1])
                if best is None or (st, idx) < (best[0], best[2]):
                    best = (st, e, idx)
            assert best is not None, "scheduler deadlock"
            st, e, idx = best
            o = ops[idx]
            o.start = st
            if o.dma is None:
                o.busy = o.est
                o.end = st + o.est
            else:
                o.busy = o.est
                t0 = max(st + o.est, dma_free)
                dma_free = t0 + o.dma / 340e3
                o.end = dma_free + 2.0
            free[e] = st + o.busy
            pend[e].remove(idx)
            left -= 1

    def emit(self):
        nc = self.nc
        handles = {"pe": nc.tensor, "act": nc.scalar, "dve": nc.vector, "pool": nc.gpsimd, "sp": nc.sync}
        sems = {e: self.es.enter_context(nc.semaphore("c_" + e)) for e in handles}
        cnt = {e: 0 for e in handles}
        waited = {e: {} for e in handles}
        order = sorted(self.ops, key=lambda o: (o.start, o.idx))
        nd = 0
        for o in order:
            e = o.eng
            h = handles[e]
            for (j, isw) in o.deps:
                d = self.ops[j]
                assert d.ev is not None, "dependency emitted after its consumer"
                s, v = d.ev
                if d.dma is None and d.eng == e and e == "pe":
                    continue
                if waited[e].get(s, 0) >= v:
                    continue
                h.wait_ge(s, v)
                waited[e][s] = v
            if o.dma is not None:
                nd += 1
                dsem = self.es.enter_context(nc.semaphore("d%d" % nd))
                out_ap, in_ap = o.calls[0]
                h.dma_start(out=out_ap, in_=in_ap).then_inc(dsem, 16)
                o.ev = (dsem, 16)
            else:
                ins = None
                for (name, a, kw) in o.calls:
                    ins = getattr(h, name)(*a, **kw)
                cnt[e] += 1
                ins.then_inc(sems[e], 1)
                o.ev = (sems[e], cnt[e])
        for idx in self.stores:
            s, v = self.ops[idx].ev
            nc.sync.wait_ge(s, v)


def build_program():
    nc = bass.Bass("TRN2", target_bir_lowering=False)
    dr = {}

    def din(name, shape):
        dr[name] = nc.dram_tensor(name, list(shape), F32, kind="ExternalInput").ap()
        return dr[name]

    x_d = din("x", [NT, D])
    xh_d = din("xh", [H, D])
    g1_d = din("g1", [128, D])
    g2_d = din("g2", [128, D])
    gf_d = din("gf", [128, D])
    win_d = din("w_in", [D, INC])
    pw_d = din("pool_w", [4, 128, 128])
    ps_d = din("pscale", [128, 4])
    cw_d = din("convw", [128, 12])
    wout_d = din("w_out", [D, D])
    wg_d = din("w_gate", [D, DFF])
    wu_d = din("w_up", [D, DFF])
    wd_d = din("w_down", [DFF, D])
    id_d = din("ident", [128, 128])
    ic_d = din("invcnt", [128, 64])
    y_d = nc.dram_tensor("y", [NT, D], F32, kind="ExternalOutput").ap()

    with ExitStack() as es:
        k = K(nc, es)
        rv, ra, rp, rt = k.rv, k.ra, k.rp, k.rt
        arena = nc.alloc_sbuf_tensor("arena", [128, ARENA], U8)
        base = nc.lookup_mloc(arena).addr
        off = [base]

        def region(nbytes):
            a = off[0]
            off[0] += (nbytes + 31) // 32 * 32
            return a

        cnt = [0]

        def at(addr, shape, dt, name):
            cnt[0] += 1
            return nc.alloc_sbuf_tensor_at("%s_%d" % (name, cnt[0]), list(shape), dt, offset=addr)

        X1 = region(NSUB * D * 4)
        HNT = region(KD * (NT + H) * 2)
        RA = region(32768)
        RB = region(32768)
        RC = region(29344)
        WOUT = region(16384)
        POOLW = region(1024)
        IDENT = region(256)
        CONVW = region(48)
        PSCALE = region(16)
        INVC = region(256)
        SS = region(256)
        MS = region(256)
        INV = region(256)
        MHALF = region(4)
        assert off[0] - base <= ARENA, off[0] - base

        x1 = at(X1, [128, NSUB, D], F32, "x1")
        hnT = at(HNT, [128, KD, NT + H], BF16, "hnT")
        w_in = at(RA, [128, KD, INC], BF16, "w_in")
        ymix = at(RB, [128, KD, NT], BF16, "ymix")
        w_out = at(WOUT, [128, KD, D], BF16, "w_out")
        poolw = at(POOLW, [128, 4, 128], BF16, "poolw")
        ident = at(IDENT, [128, 128], BF16, "ident")
        convw = at(CONVW, [128, 12], F32, "convw")
        pscale = at(PSCALE, [128, 4], F32, "pscale")
        invc = at(INVC, [128, 64], F32, "invc")
        ss = at(SS, [128, 64], F32, "ss")
        ms = at(MS, [128, 64], F32, "ms")
        inv = at(INV, [128, 64], F32, "inv")
        mhalf = at(MHALF, [128, 1], F32, "mhalf")
        hn = [at(RC + i * 2048, [128, D], BF16, "hn") for i in range(2)]
        gain = at(RC + 4096, [128, D], F32, "gain")
        xh = at(RC + 8192, [H, D], F32, "xh")
        V = [at(RC + 12288 + q * 2112, [128, TT + H], F32, "V") for q in range(2)]
        SA = at(RC + 16512, [128, TT + H], F32, "SA")
        SB = at(RC + 18624, [128, TT + H], F32, "SB")
        PL = [at(RC + 20736 + q * 1024, [128, TT], BF16, "PL") for q in range(2)]
        CS = at(RC + 22784, [128, TT], F32, "CS")
        U = at(RC + 24832, [128, TT + 2], F32, "U")
        TC = at(RC + 26944, [128, TT], F32, "TC")
        T16 = at(RC + 28992, [128, H], F32, "T16")
        Vc = at(RC + 29056, [128, 4, H], F32, "Vc")
        Uc = at(RC + 29312, [128, 4, 2], F32, "Uc")
        hT = [at(RC + q * 4096, [128, FG, TT], BF16, "hT") for q in range(2)]
        SG = [at(RC + 8192 + q * 2048, [128, TT], F32, "SG") for q in range(2)]
        ST = [at(RC + 12288 + q * 4096, [128, D], F32, "ST") for q in range(2)]
        ST.append(at(RC + 24576, [128, D], F32, "ST"))
        ST += [at(RB + 24576 + q * 4096, [128, D], F32, "ST") for q in range(2)]
        NST = len(ST)
        gfb = at(RC + 20480, [128, D], F32, "gfb")
        slot_base = [RA, RB]
        wg_s = [at(slot_base[q], [128, KD, TT], BF16, "wg") for q in range(2)]
        wu_s = [at(slot_base[q] + 8192, [128, KD, TT], BF16, "wu") for q in range(2)]
        wd_s = [at(slot_base[q] + 16384, [128, FG, D], BF16, "wd") for q in range(2)]

        psb = [es.enter_context(nc.psum_tensor("ps%d" % i, [128, TT], F32)) for i in range(7)]
        psT = es.enter_context(nc.psum_tensor("psT", [128, KD, 128], BF16))
        b_ps = [Buf("ps%d" % i) for i in range(7)]
        b_psT = Buf("psT")
        psb.append(psT.bitcast(F32).reshape([128, TT]))
        b_ps.append(b_psT)

        b_x = [Buf("x%d" % s) for s in range(NSUB)]
        b_xh = Buf("xh")
        b_hnT = [Buf("hnT%d" % s) for s in range(NSUB)]
        b_hnTh = Buf("hnTh")
        b_hn = [Buf("hn0"), Buf("hn1")]
        b_gain = Buf("gain")
        b_winP = Buf("winP")
        b_winB = [Buf("winB0"), Buf("winB1")]
        b_winC = [Buf("winC0"), Buf("winC1")]
        b_winH = [Buf("winH0"), Buf("winH1")]
        b_win_all = [b_winP] + b_winB + b_winC + b_winH
        b_wout = [Buf("wout0"), Buf("wout1")]
        b_convw, b_pscale, b_invc = Buf("convw"), Buf("pscale"), Buf("invc")
        b_ident = Buf("ident")
        b_poolw = Buf("poolw")
        b_mhalf = Buf("mhalf")
        b_ymix = [[Buf("ym%d_%d" % (t, c)) for c in range(KD)] for t in range(NTILE)]

        cs = k.dsem()
        gs = k.dsem()
        xs_sem = k.dsem()
        x_sems = [k.dsem() for _ in range(NSUB)]
        k.dma("sp", xs_sem, xh[:], xh_d[:], writes=[b_xh])
        k.dma("sp", x_sems[0], x1[:, 0, :], x_d[0:128, :], writes=[b_x[0]])
        k.dma("sp", gs, gain[:], g1_d[:], writes=[b_gain])
        for s in range(1, 4):
            k.dma("sp", x_sems[s], x1[:, s, :], x_d[s * 128:(s + 1) * 128, :], writes=[b_x[s]])
        k.dma("sp", cs, convw[:], cw_d[:], writes=[b_convw])
        k.dma("sp", cs, pscale[:], ps_d[:], writes=[b_pscale])
        k.dma("sp", cs, invc[:], ic_d[:], writes=[b_invc])
        k.op("dve", lambda: rv.memset(mhalf[:], -0.5), writes=[b_mhalf])
        b_warm = Buf("warm")
        k.op("act", lambda: ra.activation(out=ss[:, 63:64], in_=mhalf[:, 0:1], func=AF.Tanh),
             reads=[b_mhalf], writes=[b_warm])

        ids = k.dsem()
        k.dma("pool", ids, ident[:], id_d[:], writes=[b_ident])
        win_v = win_d.rearrange("(k p) c -> p k c", p=128)

        def load_mixer_weights(gate):
            k.dma("pool", k.dsem(), w_in[:, :, 0:512], win_v[:, :, 0:512], reads=[gate[0]], writes=[b_winP])
            k.dma("pool", k.dsem(), poolw[:], pw_d.rearrange("g c d -> c g d"), reads=[gate[0]], writes=[b_poolw])
            for hf in range(2):
                for (c0, bb) in ((1024, b_winC), (1536, b_winH), (512, b_winB)):
                    a = c0 + hf * 256
                    k.dma("pool", k.dsem(), w_in[:, :, a:a + 256], win_v[:, :, a:a + 256],
                          reads=[gate[1 + hf]], writes=[bb[hf]])
            first_half = [b_winP, b_winC[0], b_winH[0], b_winB[0]]
            for s in range(4, NSUB):
                k.dma("sp", x_sems[s], x1[:, s, :], x_d[s * 128:(s + 1) * 128, :],
                      reads=(first_half if s < 8 else b_win_all), writes=[b_x[s]])
        wos = k.dsem()
        wout_v = wout_d.rearrange("(k p) d -> p k d", p=128)

        def load_wout():
            for hh in range(2):
                k.dma("pool", wos, w_out[:, :, hh * 512:(hh + 1) * 512], wout_v[:, :, hh * 512:(hh + 1) * 512],
                      reads=[b_x[NSUB - 1]], writes=[b_wout[hh]])

        stat_i = [0]

        class NormJob:
            pass

        def nj_new(xsrc_ap, rows, b_xsrc, gain_ap, b_g, out_ap, b_out, junk_ap, b_jk):
            j = NormJob()
            j.x, j.rows, j.bx, j.g, j.bg = xsrc_ap, rows, b_xsrc, gain_ap, b_g
            j.out, j.bout, j.junk, j.bjk = out_ap, b_out, junk_ap, b_jk
            j.i = stat_i[0]
            stat_i[0] += 1
            j.bs, j.bm, j.bi = Buf("ss"), Buf("ms"), Buf("inv")
            return j

        def nj_square(j):
            i, rows = j.i, j.rows
            k.op("act", lambda: ra.activation(out=j.junk, in_=j.x, func=AF.Square,
                                                     accum_out=ss[:rows, i:i + 1]),
                 reads=[j.bx], writes=[j.bs], scratch=[j.bjk])

        def nj_inv(j):
            i, rows = j.i, j.rows
            k.op("dve", lambda: rv.tensor_scalar(ms[:rows, i:i + 1], ss[:rows, i:i + 1], 1.0 / D, EPS,
                                                        op0=ALU.mult, op1=ALU.add),
                 reads=[j.bs], writes=[j.bm])
            k.op("pool", lambda: rp.tensor_tensor(out=inv[:rows, i:i + 1], in0=ms[:rows, i:i + 1],
                                                         in1=mhalf[:rows, 0:1], op=ALU.pow),
                 reads=[j.bm, b_mhalf], writes=[j.bi])

        def nj_scale(j, split=False):
            i, rows = j.i, j.rows
            if split:
                k.op("act", lambda: ra.mul(out=j.out, in_=j.x, mul=inv[:rows, i:i + 1]),
                     reads=[j.bx, j.bi], writes=[j.bout])
                k.op("pool", lambda: rp.tensor_tensor(out=j.out, in0=j.out, in1=j.g, op=ALU.mult),
                     reads=[j.bout, j.bg], writes=[j.bout])
                return
            k.op("dve", lambda: rv.scalar_tensor_tensor(out=j.out, in0=j.x, scalar=inv[:rows, i:i + 1],
                                                               in1=j.g, op0=ALU.mult, op1=ALU.mult),
                 reads=[j.bx, j.bi, j.bg], writes=[j.bout])

        def norm_stage_b(rows, slot, col0, b_dst, copy_eng="act"):
            def tr():
                ins = None
                for kk in range(KD):
                    ins = rt.transpose(out=psT[:, kk, :rows], in_=hn[slot][:rows, kk * 128:(kk + 1) * 128],
                                              identity=ident[:rows, :rows])
                return ins
            k.op("pe", tr, reads=[b_hn[slot], b_ident], writes=[b_psT])
            dst = hnT[:, :, col0:col0 + rows]
            if copy_eng == "dve":
                k.op("dve", lambda: rv.tensor_copy(dst, psT[:, :, :rows]), reads=[b_psT], writes=[b_dst])
            else:
                k.op("act", lambda: ra.copy(out=dst, in_=psT[:, :, :rows]), reads=[b_psT], writes=[b_dst])

        p1_items = [("h", H)] + [(s, 128) for s in range(NSUB)]

        p1_jobs = {}
        for q in range(5):
            hn.append(at(RB + q * 2048, [128, D], BF16, "hns"))
            b_hn.append(Buf("hns%d" % q))

        def p1_slot(idx):
            return 2 + idx if idx < 5 else idx % 2

        def p1_job(idx):
            if idx not in p1_jobs:
                s_, rows = p1_items[idx]
                sl = p1_slot(idx)
                if s_ == "h":
                    p1_jobs[idx] = nj_new(xh[:rows, :], rows, b_xh, gain[:rows, :], b_gain,
                                          hn[sl][:rows, :], b_hn[sl], hn[sl][:rows, :], b_hn[sl])
                else:
                    p1_jobs[idx] = nj_new(x1[:, s_, :], rows, b_x[s_], gain[:, :], b_gain,
                                          hn[sl][:, :], b_hn[sl], hn[sl][:, :], b_hn[sl])
            return p1_jobs[idx]

        def p1_sq(idx):
            nj_square(p1_job(idx))
            nj_inv(p1_job(idx))

        def p1_sc(idx):
            nj_scale(p1_job(idx))

        def p1_b(idx):
            s_, rows = p1_items[idx]
            ce = "dve" if idx in (2, 4) else "act"
            if s_ == "h":
                norm_stage_b(rows, p1_slot(idx), 0, b_hnTh, ce)
            else:
                norm_stage_b(rows, p1_slot(idx), H + s_ * 128, b_hnT[s_], ce)

        b_V = [Buf("V0"), Buf("V1")]
        b_SA, b_SB = Buf("SA"), Buf("SB")
        b_PL = [Buf("PL0"), Buf("PL1")]
        b_CS, b_TC, b_T16, b_U = Buf("CS"), Buf("TC"), Buf("T16"), Buf("U")
        b_Vc = [Buf("Vc%d" % i) for i in range(4)]
        b_Uc = [Buf("Uc%d" % i) for i in range(4)]
        p1_bufs = b_hn[0:2] + [b_gain, b_xh]
        p2_bufs = b_V + [b_SA, b_SB] + b_PL + [b_CS, b_TC, b_T16, b_U] + b_Vc + b_Uc

        rot = {}

        def next_bank(role_banks, key):
            i = rot.get(key, 0)
            rot[key] = i + 1
            return role_banks[i % len(role_banks)]

        def hn_reads(t):
            return [b_hnT[4 * t + j] for j in range(4)]

        def inproj_pairs(col0, ncols, tok0, ntok):
            return [(w_in[:, kk, col0:col0 + ncols], hnT[:, kk, tok0:tok0 + ntok]) for kk in range(KD)]

        M_banks = [0, 1, 2, 3, 4, 5, 6]
        pcount = [0]

        def pool_front(gi, t):
            q = pcount[0] % 2
            pcount[0] += 1
            w = WINS[gi]
            c0 = gi * 128
            if t == 0:
                bk = next_bank(M_banks, "m")
                k.mm(psb[bk][:, 0:H], inproj_pairs(c0, 128, 0, H), reads=[b_winP, b_hnTh], writes=[b_ps[bk]])
                k.op("act", lambda: ra.copy(out=V[q][:, 0:H], in_=psb[bk][:, 0:H]),
                     reads=[b_ps[bk]], writes=[b_V[q]])
            else:
                k.op("act", lambda: ra.copy(out=V[q][:, 0:H], in_=Vc[:, gi, :]),
                     reads=[b_Vc[gi]], writes=[b_V[q]])
            bk = next_bank(M_banks, "m")
            k.mm(psb[bk][:, :], inproj_pairs(c0, 128, H + t * TT, TT), reads=[b_winP] + hn_reads(t),
                 writes=[b_ps[bk]])
            k.op("act", lambda: ra.copy(out=V[q][:, H:H + TT], in_=psb[bk][:, :]),
                 reads=[b_ps[bk]], writes=[b_V[q]])
            if t < NTILE - 1:
                k.op("act", lambda: ra.copy(out=Vc[:, gi, :], in_=V[q][:, TT:TT + H]),
                     reads=[b_V[q]], writes=[b_Vc[gi]])
            L = TT + H
            k.op("pool", lambda: rp.tensor_tensor(out=SA[:, 1:L], in0=V[q][:, 1:L], in1=V[q][:, 0:L - 1],
                                                         op=ALU.add), reads=[b_V[q]], writes=[b_SA])
            S, bS = SA, b_SA
            if w >= 4:
                k.op("pool", lambda: rp.tensor_tensor(out=SB[:, 3:L], in0=SA[:, 3:L], in1=SA[:, 1:L - 2],
                                                             op=ALU.add), reads=[b_SA], writes=[b_SB])
                S, bS = SB, b_SB
            if w >= 8:
                k.op("pool", lambda: rp.tensor_tensor(out=SA[:, 7:L], in0=SB[:, 7:L], in1=SB[:, 3:L - 4],
                                                             op=ALU.add), reads=[b_SB], writes=[b_SA])
                S, bS = SA, b_SA
            if w >= 16:
                k.op("pool", lambda: rp.tensor_tensor(out=SB[:, 15:L], in0=SA[:, 15:L], in1=SA[:, 7:L - 8],
                                                             op=ALU.add), reads=[b_SA], writes=[b_SB])
                S, bS = SB, b_SB
            k.op("dve", lambda: rv.scalar_tensor_tensor(out=PL[q][:, :], in0=S[:, H:L], scalar=1.0 / w,
                                                               in1=V[q][:, H:L], op0=ALU.mult, op1=ALU.subtract),
                 reads=[bS, b_V[q]], writes=[b_PL[q]])
            if t == 0:
                k.op("dve", lambda: rv.tensor_tensor(out=T16[:, :], in0=S[:, H:2 * H],
                                                            in1=invc[:, gi * H:(gi + 1) * H], op=ALU.mult),
                     reads=[bS, b_invc], writes=[b_T16])
                k.op("dve", lambda: rv.tensor_tensor(out=PL[q][:, 0:H], in0=T16[:, :], in1=V[q][:, H:2 * H],
                                                            op=ALU.subtract),
                     reads=[b_T16, b_V[q], b_PL[q]], writes=[b_PL[q]])

            def back():
                bk2 = next_bank(M_banks, "m")
                k.mm(psb[bk2][:, :], [(poolw[:, gi, :], PL[q][:, :])], reads=[b_poolw, b_PL[q]], writes=[b_ps[bk2]])
                k.op("act", lambda: ra.mul(out=ymix[:, gi, t * TT:(t + 1) * TT], in_=psb[bk2][:, :],
                                                  mul=pscale[:, gi:gi + 1]),
                     reads=[b_ps[bk2], b_pscale], writes=[b_ymix[t][gi]])
            return back

        def conv_front(j, t):
            hf = j // 2
            cB, cC, ch = 512 + j * 128, 1024 + j * 128, 1536 + j * 128
            if t == 0:
                bkc = next_bank(M_banks, "m")
                k.mm(psb[bkc][:, 0:2], inproj_pairs(cC, 128, H - 2, 2), reads=[b_winC[hf], b_hnTh],
                     writes=[b_ps[bkc]])
                bkh = next_bank(M_banks, "m")
                k.mm(psb[bkh][:, 0:2], inproj_pairs(ch, 128, H - 2, 2), reads=[b_winH[hf], b_hnTh],
                     writes=[b_ps[bkh]])
                k.op("act", lambda: ra.copy(out=CS[:, 0:2], in_=psb[bkc][:, 0:2]),
                     reads=[b_ps[bkc]], writes=[b_CS])
                k.op("dve", lambda: rv.tensor_tensor(out=U[:, 0:2], in0=CS[:, 0:2], in1=psb[bkh][:, 0:2],
                                                            op=ALU.mult),
                     reads=[b_CS, b_ps[bkh]], writes=[b_U])
            else:
                k.op("act", lambda: ra.copy(out=U[:, 0:2], in_=Uc[:, j, :]),
                     reads=[b_Uc[j]], writes=[b_U])
            bkc = next_bank(M_banks, "m")
            k.mm(psb[bkc][:, :], inproj_pairs(cC, 128, H + t * TT, TT), reads=[b_winC[hf]] + hn_reads(t),
                 writes=[b_ps[bkc]])
            bkh = next_bank(M_banks, "m")
            k.mm(psb[bkh][:, :], inproj_pairs(ch, 128, H + t * TT, TT), reads=[b_winH[hf]] + hn_reads(t),
                 writes=[b_ps[bkh]])
            bkb = next_bank(M_banks, "m")
            k.mm(psb[bkb][:, :], inproj_pairs(cB, 128, H + t * TT, TT), reads=[b_winB[hf]] + hn_reads(t),
                 writes=[b_ps[bkb]])
            k.op("act", lambda: ra.copy(out=CS[:, :], in_=psb[bkc][:, :]), reads=[b_ps[bkc]], writes=[b_CS])
            k.op("dve", lambda: rv.tensor_tensor(out=U[:, 2:TT + 2], in0=CS[:, :], in1=psb[bkh][:, :],
                                                        op=ALU.mult),
                 reads=[b_CS, b_ps[bkh], b_U], writes=[b_U])
            k.op("act", lambda: ra.mul(out=TC[:, :], in_=U[:, 2:TT + 2], mul=convw[:, j * 3 + 2:j * 3 + 3]),
                 reads=[b_U, b_convw], writes=[b_TC])
            if t < NTILE - 1:
                k.op("act", lambda: ra.copy(out=Uc[:, j, :], in_=U[:, TT:TT + 2]),
                     reads=[b_U], writes=[b_Uc[j]])
            k.op("dve", lambda: rv.scalar_tensor_tensor(out=TC[:, :], in0=U[:, 1:TT + 1],
                                                               scalar=convw[:, j * 3 + 1:j * 3 + 2], in1=TC[:, :],
                                                               op0=ALU.mult, op1=ALU.add),
                 reads=[b_U, b_convw, b_TC], writes=[b_TC])
            k.op("dve", lambda: rv.scalar_tensor_tensor(out=TC[:, :], in0=U[:, 0:TT],
                                                               scalar=convw[:, j * 3:j * 3 + 1], in1=TC[:, :],
                                                               op0=ALU.mult, op1=ALU.add),
                 reads=[b_U, b_convw, b_TC], writes=[b_TC])
            k.op("dve", lambda: rv.tensor_tensor(out=ymix[:, 4 + j, t * TT:(t + 1) * TT], in0=TC[:, :],
                                                        in1=psb[bkb][:, :], op=ALU.mult),
                 reads=[b_TC, b_ps[bkb]], writes=[b_ymix[t][4 + j]])
            return None

        for idx in range(5):
            p1_sq(idx)
        load_mixer_weights([p1_job(GATE_JOBS[0]).bm, p1_job(GATE_JOBS[1]).bm, p1_job(GATE_JOBS[2]).bm])
        for idx in range(5):
            p1_sc(idx)
        for idx in range(5):
            p1_b(idx)
        ev_hns = retire(*b_hn[2:7])
        for tt_ in range(NTILE):
            for cc in range(3):
                inherit(b_ymix[tt_][cc], ev_hns)
        for t in range(NTILE):
            if t == 0:
                order = [("p", 0), ("p", 1), ("p", 2), ("p", 3), ("c", 0), ("c", 1), ("c", 2), ("c", 3)]
            else:
                order = [("p", 0), ("c", 0), ("p", 1), ("c", 1), ("p", 2), ("c", 2), ("p", 3), ("c", 3)]
            pending = []
            nb = 1 + 4 * (t + 1)
            if t + 1 < NTILE:
                p1_sq(nb)
            for i, (kind, a) in enumerate(order):
                back = pool_front(a, t) if kind == "p" else conv_front(a, t)
                if kind == "p" and pending:
                    pending.pop(0)()
                if back is not None:
                    pending.append(back)
                if t + 1 < NTILE:
                    kq = i // 2
                    if i % 2 == 0:
                        p1_sc(nb + kq)
                        if i == 0:
                            p1_sq(nb + 1)
                    else:
                        p1_b(nb + kq)
                        if 1 <= kq <= 2:
                            p1_sq(nb + kq + 1)
            while pending:
                pending.pop(0)()
            if t == 0:
                load_wout()

        ffn_sems = [k.dsem(), k.dsem()]
        b_wg = [Buf("wg0"), Buf("wg1")]
        b_wu = [Buf("wu0"), Buf("wu1")]
        b_wd = [Buf("wd0"), Buf("wd1")]
        wg_v = wg_d.rearrange("(k p) f -> p k f", p=128)
        wu_v = wu_d.rearrange("(k p) f -> p k f", p=128)
        wd_v = wd_d.rearrange("(c p) d -> p c d", p=128)

        def load_group(g):
            q = g % 2
            c0, ncn = GROUPS[g]
            f0, nf = c0 * 128, ncn * 128
            ds = ffn_sems[q]
            k.dma("pool", ds, wg_s[q][:, :, 0:nf], wg_v[:, :, f0:f0 + nf], writes=[b_wg[q]])
            k.dma("pool", ds, wu_s[q][:, :, 0:nf], wu_v[:, :, f0:f0 + nf], writes=[b_wu[q]])
            k.dma("pool", ds, wd_s[q][:, 0:ncn, :], wd_v[:, c0:c0 + ncn, :], writes=[b_wd[q]])

        ev_ra = retire(*b_win_all)
        for b in (b_wg[0], b_wu[0], b_wd[0]):
            inherit(b, ev_ra)
        load_group(0)

        k.dma("sp", gs, gain[:], g2_d[:], writes=[b_gain])
        O_banks = [0, 1, 2, 3, 4, 5]
        rot["o"] = 0

        def p3_front(s, halves):
            t, s4 = s // 4, s % 4
            for hh in halves:
                bk = next_bank(O_banks, "o")
                pairs = [(ymix[:, kk, s * 128:(s + 1) * 128], w_out[:, kk, hh * 512:(hh + 1) * 512])
                         for kk in range(KD)]
                k.mm(psb[bk][:, :], pairs, reads=[b_wout[hh]] + b_ymix[t], writes=[b_ps[bk]])
                k.op("dve", lambda: rv.tensor_tensor(out=x1[:, s, hh * 512:(hh + 1) * 512],
                                                            in0=x1[:, s, hh * 512:(hh + 1) * 512],
                                                            in1=psb[bk][:, :], op=ALU.add),
                     reads=[b_x[s], b_ps[bk]], writes=[b_x[s]])

        junk3 = at(RA + 24576, [128, D], BF16, "junk3")
        b_junk3 = Buf("junk3")
        inherit(b_junk3, ev_ra)
        for q in range(2):
            hn.append(at(RA + 26624 + q * 2048, [128, D], BF16, "hnx"))
            bq = Buf("hn%d" % (2 + q))
            inherit(bq, ev_ra)
            b_hn.append(bq)
        p3_slots = [0, 1, len(hn) - 2, len(hn) - 1]
        p3_jobs = [nj_new(x1[:, s, :], 128, b_x[s], gain[:, :], b_gain, hn[p3_slots[s % 4]][:, :],
                          b_hn[p3_slots[s % 4]], junk3[:, :], b_junk3) for s in range(NSUB)]
        for i in range(NSUB + 3):
            if i < NSUB:
                p3_front(i, (0,))
                p3_front(i, (1,))
                nj_square(p3_jobs[i])
            if 0 <= i - 1 < NSUB:
                nj_inv(p3_jobs[i - 1])
            if 0 <= i - 2 < NSUB:
                nj_scale(p3_jobs[i - 2])
            if 0 <= i - 3 < NSUB:
                norm_stage_b(128, p3_slots[(i - 3) % 4], H + (i - 3) * 128, b_hnT[i - 3])

        ev_rb = retire(*[b for row in b_ymix for b in row])
        for b in (b_wg[1], b_wu[1], b_wd[1]):
            inherit(b, ev_rb)
        load_group(1)

        ev_p3 = retire(*(p1_bufs + p2_bufs))
        b_hT = [Buf("hT0"), Buf("hT1")]
        b_SG = [Buf("SG0"), Buf("SG1")]
        b_ST = [Buf("ST%d" % q) for q in range(NST)]
        b_gf = Buf("gf")
        for b in b_hT + b_SG + b_ST + [b_gf]:
            inherit(b, ev_p3)
        for b in b_ST[3:]:
            inherit(b, ev_rb)
        gfs = k.dsem()
        k.dma("sp", gfs, gfb[:], gf_d[:], writes=[b_gf])
        st_sems = [k.dsem(), k.dsem()]

        G_banks = [0, 1]
        U_banks = [2, 3]
        D_banks = [4, 5, 6, 7]
        ffn_items = [(g, t) for g in range(len(GROUPS)) for t in range(NTILE)]

        def ffn_gu(n, c):
            g, t = ffn_items[n]
            q, hq = g % 2, n % 2
            bg = next_bank(G_banks, "g")
            bu = next_bank(U_banks, "u")
            tok0 = H + t * TT
            k.mm(psb[bg][:, :], [(wg_s[q][:, kk, c * 128:(c + 1) * 128], hnT[:, kk, tok0:tok0 + TT]) for kk in range(KD)],
                 reads=[b_wg[q]] + hn_reads(t), writes=[b_ps[bg]])
            k.mm(psb[bu][:, :], [(wu_s[q][:, kk, c * 128:(c + 1) * 128], hnT[:, kk, tok0:tok0 + TT]) for kk in range(KD)],
                 reads=[b_wu[q]] + hn_reads(t), writes=[b_ps[bu]])
            sq = next_bank([0, 1], "sg")
            k.op("act", lambda: ra.activation(out=SG[sq][:, :], in_=psb[bg][:, :], func=AF.Tanh, scale=0.5),
                 reads=[b_ps[bg]], writes=[b_SG[sq]])
            k.op("dve", lambda: rv.scalar_tensor_tensor(out=SG[sq][:, :], in0=SG[sq][:, :], scalar=1.0,
                                                        in1=psb[bg][:, :], op0=ALU.add, op1=ALU.mult),
                 reads=[b_SG[sq], b_ps[bg]], writes=[b_SG[sq]])
            k.op("dve", lambda: rv.scalar_tensor_tensor(out=hT[hq][:, c, :], in0=SG[sq][:, :], scalar=0.5,
                                                        in1=psb[bu][:, :], op0=ALU.mult, op1=ALU.mult),
                 reads=[b_SG[sq], b_ps[bu], b_hT[hq]], writes=[b_hT[hq]])

        def ffn_down(n, dsel):
            g, t = ffn_items[n]
            q, hq = g % 2, n % 2
            ncn = GROUPS[g][1]
            for s4 in (dsel // 2,):
                s = 4 * t + s4
                for hh in (dsel % 2,):
                    bk = next_bank(D_banks, "d")
                    pairs = [(hT[hq][:, c, s4 * 128:(s4 + 1) * 128], wd_s[q][:, c, hh * 512:(hh + 1) * 512])
                             for c in range(ncn)]
                    k.mm(psb[bk][:, :], pairs, reads=[b_hT[hq], b_wd[q]], writes=[b_ps[bk]])
                    k.op("dve", lambda: rv.tensor_tensor(out=x1[:, s, hh * 512:(hh + 1) * 512],
                                                                in0=x1[:, s, hh * 512:(hh + 1) * 512],
                                                                in1=psb[bk][:, :], op=ALU.add),
                         reads=[b_x[s], b_ps[bk]], writes=[b_x[s]])
                if g == len(GROUPS) - 1 and dsel % 2 == 1:
                    final_step(s)

        fin_jobs = [nj_new(x1[:, s, :], 128, b_x[s], gfb[:, :], b_gf, ST[s % NST][:, :], b_ST[s % NST],
                           junk3[:, :], b_junk3) for s in range(NSUB)]

        def final_step(i):
            if i < NSUB:
                nj_square(fin_jobs[i])
            if 0 <= i - 1 < NSUB:
                nj_inv(fin_jobs[i - 1])
            if 0 <= i - 2 < NSUB:
                s2 = i - 2
                nj_scale(fin_jobs[s2], split=FINAL_SPLIT)
                k.dma("sp", None, y_d[s2 * 128:(s2 + 1) * 128, :], ST[s2 % NST][:, :], reads=[b_ST[s2 % NST]],
                      store=True)

        nitems = len(ffn_items)
        for c in range(GROUPS[0][1]):
            ffn_gu(0, c)
        for n in range(nitems):
            g, t = ffn_items[n]
            nxt = n + 1
            gn = GROUPS[ffn_items[nxt][0]][1] if nxt < nitems else 0
            if gn:
                ffn_gu(nxt, 0)
            for dsel in range(8):
                ffn_down(n, dsel)
                c = (dsel + 1) // 2
                if dsel % 2 == 1 and 1 <= c < gn:
                    ffn_gu(nxt, c)
            if t == NTILE - 1 and g + 2 < len(GROUPS):
                load_group(g + 2)

        final_step(NSUB)
        final_step(NSUB + 1)
        k.schedule()
        k.emit()
    return nc


def _prep_inputs(inputs):
    f = lambda a: np.ascontiguousarray(np.asarray(a, dtype=np.float32))
    x = f(inputs["x"])
    rep = lambda g: np.ascontiguousarray(np.broadcast_to(f(g)[None, :], (128, D)))
    shared = {
        "g1": rep(inputs["norm1_g"]),
        "g2": rep(inputs["norm2_g"]),
        "gf": rep(inputs["normf_g"]),
        "w_in": f(inputs["w_in"]),
        "pool_w": f(inputs["pool_w"]),
        "pscale": np.ascontiguousarray(f(inputs["pool_scale"]).reshape(4, 128).T),
        "convw": np.ascontiguousarray(f(inputs["conv_w"]).reshape(3, 4, 128).transpose(2, 1, 0).reshape(128, 12)),
        "w_out": f(inputs["w_out"]),
        "w_gate": f(inputs["w_gate"]),
        "w_up": f(inputs["w_up"]),
        "w_down": f(inputs["w_down"]),
        "ident": np.eye(128, dtype=np.float32),
    }
    ic_start = np.zeros((4, H), np.float32)
    ic_mid = np.zeros((4, H), np.float32)
    for gi, w in enumerate(WINS):
        for t in range(H):
            ic_start[gi, t] = 1.0 / min(t + 1, w)
            ic_mid[gi, t] = 1.0 / w
    in_maps = []
    for c in range(N_CORES):
        b, qq = divmod(c, 4)
        t0 = qq * NT
        m = dict(shared)
        m["x"] = np.ascontiguousarray(x[b, t0:t0 + NT, :])
        if qq == 0:
            m["xh"] = np.zeros((H, D), np.float32)
            ic = ic_start
        else:
            m["xh"] = np.ascontiguousarray(x[b, t0 - H:t0, :])
            ic = ic_mid
        m["invcnt"] = np.ascontiguousarray(np.broadcast_to(ic.reshape(1, 4 * H), (128, 4 * H)))
        in_maps.append(m)
    return in_maps


def kernel(**inputs):
    in_maps = _prep_inputs(inputs)
    nc = build_program()
    res = run_bass_kernel_spmd(nc, in_maps, core_ids=list(range(N_CORES)))
    out = np.empty((2, 4 * NT, D), np.float32)
    for c in range(N_CORES):
        b, qq = divmod(c, 4)
        out[b, qq * NT:(qq + 1) * NT, :] = np.asarray(res.results[c]["y"], dtype=np.float32)
    return out
```

```python
from contextlib import ExitStack

import numpy as np
import concourse.bass as bass
import concourse.mybir as mybir
from concourse.bass_utils import run_bass_kernel_spmd

F32 = mybir.dt.float32
BF16 = mybir.dt.bfloat16
U8 = mybir.dt.uint8
ALU = mybir.AluOpType
AF = mybir.ActivationFunctionType

D = 1024
KD = 8
NT = 2048
NSUB = 16
NTILE = 4
TT = 512
H = 16
INC = 2048
DFF = 2816
EPS = 1e-6
WINS = (2, 4, 8, 16)
N_CORES = 8
FG = 4
GROUPS = [(0, 3), (3, 3), (6, 4), (10, 4), (14, 4), (18, 4)]
ARENA = 212256
GATE_JOBS = (0, 1, 2)
FINAL_SPLIT = False


class Buf:
    __slots__ = ("name", "w", "r")

    def __init__(self, name):
        self.name = name
        self.w = []
        self.r = []


def retire(*bufs):
    out = set()
    for b in bufs:
        out.update(b.w)
        out.update(b.r)
    return sorted(out)


def inherit(buf, events):
    buf.r = buf.r + list(events)


class _Rec:
    def __init__(self, sink):
        self._sink = sink

    def __getattr__(self, name):
        def f(*a, **kw):
            self._sink.append((name, a, kw))
            return None
        return f


class Op:
    __slots__ = ("idx", "eng", "calls", "deps", "est", "dma", "start", "end", "busy", "ev")


def _fsize(ap):
    n = 1
    for d in list(ap.shape)[1:]:
        n *= int(d)
    return n


def _is_psum(ap):
    try:
        return "psum" in str(ap.space).lower()
    except Exception:
        return False


class K:
    LAT = 0.25
    WINDOW = 40
    PATIENCE = 0.3

    def __init__(self, nc, es):
        self.nc = nc
        self.es = es
        self.ops = []
        self.calls = []
        self.rv = _Rec(self.calls)
        self.ra = _Rec(self.calls)
        self.rp = _Rec(self.calls)
        self.rt = _Rec(self.calls)
        self.stores = []

    def dsem(self):
        return None

    def _record(self, en, calls, reads, writes, scratch, est, dma=None):
        deps = {}
        for b in reads:
            for j in b.w:
                deps[j] = True
        for b in writes:
            for j in b.w:
                deps[j] = True
            for j in b.r:
                deps.setdefault(j, False)
        for b in scratch:
            for j in b.w + b.r:
                deps.setdefault(j, False)
        o = Op()
        o.idx = len(self.ops)
        o.eng, o.calls, o.deps, o.est, o.dma = en, calls, sorted(deps.items()), est, dma
        o.start = o.end = o.busy = o.ev = None
        self.ops.append(o)
        for b in list(writes) + list(scratch):
            b.w = [o.idx]
            b.r = []
        for b in reads:
            b.r = b.r + [o.idx]
        return o.idx

    def _cost(self, en, calls):
        t = 0.0
        for (name, a, kw) in calls:
            if en == "pe":
                if name == "transpose":
                    t += 0.075
                else:
                    rhs = kw.get("rhs", a[2] if len(a) > 2 else None)
                    n = _fsize(rhs)
                    t += 0.219 * n / 512.0 if n >= 128 else 0.035
                continue
            out = kw.get("out", a[0] if a else None)
            n = _fsize(out)
            aps = [v for v in list(a) + list(kw.values()) if hasattr(v, "shape") and hasattr(v, "space")]
            ps = any(_is_psum(v) for v in aps)
            if en == "act":
                t += 0.22 + n / 1200.0 + (0.1 if "accum_out" in kw else 0.0)
            elif en == "dve":
                if name == "tensor_copy" and ps:
                    t += 0.16 + n / 1920.0
                else:
                    t += (0.22 if name == "scalar_tensor_tensor" else 0.10) + n / 960.0 + (0.06 if ps else 0.0)
            else:
                t += 0.55 if n <= 4 else 0.75 + n / 850.0
        return t

    def op(self, en, fn, reads=(), writes=(), scratch=()):
        del self.calls[:]
        fn()
        calls = list(self.calls)
        del self.calls[:]
        return self._record(en, calls, reads, writes, scratch, self._cost(en, calls))

    def mm(self, out_ap, pairs, reads, writes):
        n = len(pairs)
        calls = [("matmul", (out_ap, l, r), {"start": i == 0, "stop": i == n - 1}) for i, (l, r) in enumerate(pairs)]
        return self._record("pe", calls, reads, writes, (), self._cost("pe", calls))

    def dma(self, qn, ds, out_ap, in_ap, reads=(), writes=(), store=False):
        nbytes = 4 * 128 * 0
        try:
            nbytes = int(in_ap.nbytes) if not store else int(out_ap.nbytes)
        except Exception:
            nbytes = 4 * int(np.prod([int(d) for d in in_ap.shape]))
        idx = self._record(qn, [(out_ap, in_ap)], reads, writes, (), 0.45 if qn == "sp" else 1.06, dma=nbytes)
        if store:
            self.stores.append(idx)
        return idx

    def schedule(self):
        ops = self.ops
        engs = ["pe", "act", "dve", "pool", "sp"]
        pend = {e: [o.idx for o in ops if o.eng == e] for e in engs}
        free = {e: 0.0 for e in engs}
        dma_free = 0.0
        left = len(ops)
        while left:
            best = None
            for e in engs:
                cands = []
                for idx in pend[e][:self.WINDOW]:
                    o = ops[idx]
                    ready = 0.0
                    ok = True
                    for (j, _) in o.deps:
                        d = ops[j]
                        if d.end is None:
                            ok = False
                            break
                        lat = 0.0 if (d.eng == e and d.dma is None) else self.LAT
                        if d.end + lat > ready:
                            ready = d.end + lat
                    if ok:
                        cands.append((max(ready, free[e]), idx))
                if not cands:
                    continue
                mn = min(c[0] for c in cands)
                st, idx = min(((c[0], c[1]) for c in cands if c[0] <= mn + self.PATIENCE), key=lambda c: c[1])
                if best is None or (st, idx) < (best[0], best[2]):
                    best = (st, e, idx)
            assert best is not None, "scheduler deadlock"
            st, e, idx = best
            o = ops[idx]
            o.start = st
            if o.dma is None:
                o.busy = o.est
                o.end = st + o.est
            else:
                o.busy = o.est
                t0 = max(st + o.est, dma_free)
                dma_free = t0 + o.dma / 340e3
                o.end = dma_free + 2.0
            free[e] = st + o.busy
            pend[e].remove(idx)
            left -= 1

    def emit(self):
        nc = self.nc
        handles = {"pe": nc.tensor, "act": nc.scalar, "dve": nc.vector, "pool": nc.gpsimd, "sp": nc.sync}
        sems = {e: self.es.enter_context(nc.semaphore("c_" + e)) for e in handles}
        cnt = {e: 0 for e in handles}
        waited = {e: {} for e in handles}
        order = sorted(self.ops, key=lambda o: (o.start, o.idx))
        nd = 0
        for o in order:
            e = o.eng
            h = handles[e]
            for (j, isw) in o.deps:
                d = self.ops[j]
                assert d.ev is not None, "dependency emitted after its consumer"
                s, v = d.ev
                if d.dma is None and d.eng == e and e == "pe":
                    continue
                if waited[e].get(s, 0) >= v:
                    continue
                h.wait_ge(s, v)
                waited[e][s] = v
            if o.dma is not None:
                nd += 1
                dsem = self.es.enter_context(nc.semaphore("d%d" % nd))
                out_ap, in_ap = o.calls[0]
                h.dma_start(out=out_ap, in_=in_ap).then_inc(dsem, 16)
                o.ev = (dsem, 16)
            else:
                ins = None
                for (name, a, kw) in o.calls:
                    ins = getattr(h, name)(*a, **kw)
                cnt[e] += 1
                ins.then_inc(sems[e], 1)
                o.ev = (sems[e], cnt[e])
        for idx in self.stores:
            s, v = self.ops[idx].ev
            nc.sync.wait_ge(s, v)


def build_program():
    nc = bass.Bass("TRN2", target_bir_lowering=False)
    dr = {}

    def din(name, shape):
        dr[name] = nc.dram_tensor(name, list(shape), F32, kind="ExternalInput").ap()
        return dr[name]

    x_d = din("x", [NT, D])
    xh_d = din("xh", [H, D])
    g1_d = din("g1", [128, D])
    g2_d = din("g2", [128, D])
    gf_d = din("gf", [128, D])
    win_d = din("w_in", [D, INC])
    pw_d = din("pool_w", [4, 128, 128])
    ps_d = din("pscale", [128, 4])
    cw_d = din("convw", [128, 12])
    wout_d = din("w_out", [D, D])
    wg_d = din("w_gate", [D, DFF])
    wu_d = din("w_up", [D, DFF])
    wd_d = din("w_down", [DFF, D])
    id_d = din("ident", [128, 128])
    ic_d = din("invcnt", [128, 64])
    y_d = nc.dram_tensor("y", [NT, D], F32, kind="ExternalOutput").ap()

    with ExitStack() as es:
        k = K(nc, es)
        rv, ra, rp, rt = k.rv, k.ra, k.rp, k.rt
        arena = nc.alloc_sbuf_tensor("arena", [128, ARENA], U8)
        base = nc.lookup_mloc(arena).addr
        off = [base]

        def region(nbytes):
            a = off[0]
            off[0] += (nbytes + 31) // 32 * 32
            return a

        cnt = [0]

        def at(addr, shape, dt, name):
            cnt[0] += 1
            return nc.alloc_sbuf_tensor_at("%s_%d" % (name, cnt[0]), list(shape), dt, offset=addr)

        X1 = region(NSUB * D * 4)
        HNT = region(KD * (NT + H) * 2)
        RA = region(32768)
        RB = region(32768)
        RC = region(29344)
        WOUT = region(16384)
        POOLW = region(1024)
        IDENT = region(256)
        CONVW = region(48)
        PSCALE = region(16)
        INVC = region(256)
        SS = region(256)
        MS = region(256)
        INV = region(256)
        MHALF = region(4)
        assert off[0] - base <= ARENA, off[0] - base

        x1 = at(X1, [128, NSUB, D], F32, "x1")
        hnT = at(HNT, [128, KD, NT + H], BF16, "hnT")
        w_in = at(RA, [128, KD, INC], BF16, "w_in")
        ymix = at(RB, [128, KD, NT], BF16, "ymix")
        w_out = at(WOUT, [128, KD, D], BF16, "w_out")
        poolw = at(POOLW, [128, 4, 128], BF16, "poolw")
        ident = at(IDENT, [128, 128], BF16, "ident")
        convw = at(CONVW, [128, 12], F32, "convw")
        pscale = at(PSCALE, [128, 4], F32, "pscale")
        invc = at(INVC, [128, 64], F32, "invc")
        ss = at(SS, [128, 64], F32, "ss")
        ms = at(MS, [128, 64], F32, "ms")
        inv = at(INV, [128, 64], F32, "inv")
        mhalf = at(MHALF, [128, 1], F32, "mhalf")
        hn = [at(RC + i * 2048, [128, D], BF16, "hn") for i in range(2)]
        gain = at(RC + 4096, [128, D], F32, "gain")
        xh = at(RC + 8192, [H, D], F32, "xh")
        V = [at(RC + 12288 + q * 2112, [128, TT + H], F32, "V") for q in range(2)]
        SA = at(RC + 16512, [128, TT + H], F32, "SA")
        SB = at(RC + 18624, [128, TT + H], F32, "SB")
        PL = [at(RC + 20736 + q * 1024, [128, TT], BF16, "PL") for q in range(2)]
        CS = at(RC + 22784, [128, TT], F32, "CS")
        U = at(RC + 24832, [128, TT + 2], F32, "U")
        TC = at(RC + 26944, [128, TT], F32, "TC")
        T16 = at(RC + 28992, [128, H], F32, "T16")
        Vc = at(RC + 29056, [128, 4, H], F32, "Vc")
        Uc = at(RC + 29312, [128, 4, 2], F32, "Uc")
        hT = [at(RC + q * 4096, [128, FG, TT], BF16, "hT") for q in range(2)]
        SG = [at(RC + 8192 + q * 2048, [128, TT], F32, "SG") for q in range(2)]
        ST = [at(RC + 12288 + q * 4096, [128, D], F32, "ST") for q in range(2)]
        ST.append(at(RC + 24576, [128, D], F32, "ST"))
        ST += [at(RB + 24576 + q * 4096, [128, D], F32, "ST") for q in range(2)]
        NST = len(ST)
        gfb = at(RC + 20480, [128, D], F32, "gfb")
        slot_base = [RA, RB]
        wg_s = [at(slot_base[q], [128, KD, TT], BF16, "wg") for q in range(2)]
        wu_s = [at(slot_base[q] + 8192, [128, KD, TT], BF16, "wu") for q in range(2)]
        wd_s = [at(slot_base[q] + 16384, [128, FG, D], BF16, "wd") for q in range(2)]

        psb = [es.enter_context(nc.psum_tensor("ps%d" % i, [128, TT], F32)) for i in range(7)]
        psT = es.enter_context(nc.psum_tensor("psT", [128, KD, 128], BF16))
        b_ps = [Buf("ps%d" % i) for i in range(7)]
        b_psT = Buf("psT")
        psb.append(psT.bitcast(F32).reshape([128, TT]))
        b_ps.append(b_psT)

        b_x = [Buf("x%d" % s) for s in range(NSUB)]
        b_xh = Buf("xh")
        b_hnT = [Buf("hnT%d" % s) for s in range(NSUB)]
        b_hnTh = Buf("hnTh")
        b_hn = [Buf("hn0"), Buf("hn1")]
        b_gain = Buf("gain")
        b_winP = Buf("winP")
        b_winB = [Buf("winB0"), Buf("winB1")]
        b_winC = [Buf("winC0"), Buf("winC1")]
        b_winH = [Buf("winH0"), Buf("winH1")]
        b_win_all = [b_winP] + b_winB + b_winC + b_winH
        b_wout = [Buf("wout0"), Buf("wout1")]
        b_convw, b_pscale, b_invc = Buf("convw"), Buf("pscale"), Buf("invc")
        b_ident = Buf("ident")
        b_poolw = Buf("poolw")
        b_mhalf = Buf("mhalf")
        b_ymix = [[Buf("ym%d_%d" % (t, c)) for c in range(KD)] for t in range(NTILE)]

        cs = k.dsem()
        gs = k.dsem()
        xs_sem = k.dsem()
        x_sems = [k.dsem() for _ in range(NSUB)]
        k.dma("sp", xs_sem, xh[:], xh_d[:], writes=[b_xh])
        k.dma("sp", x_sems[0], x1[:, 0, :], x_d[0:128, :], writes=[b_x[0]])
        k.dma("sp", gs, gain[:], g1_d[:], writes=[b_gain])
        for s in range(1, 4):
            k.dma("sp", x_sems[s], x1[:, s, :], x_d[s * 128:(s + 1) * 128, :], writes=[b_x[s]])
        k.dma("sp", cs, convw[:], cw_d[:], writes=[b_convw])
        k.dma("sp", cs, pscale[:], ps_d[:], writes=[b_pscale])
        k.dma("sp", cs, invc[:], ic_d[:], writes=[b_invc])
        k.op("dve", lambda: rv.memset(mhalf[:], -0.5), writes=[b_mhalf])
        b_warm = Buf("warm")
        k.op("act", lambda: ra.activation(out=ss[:, 63:64], in_=mhalf[:, 0:1], func=AF.Tanh),
             reads=[b_mhalf], writes=[b_warm])

        ids = k.dsem()
        k.dma("pool", ids, ident[:], id_d[:], writes=[b_ident])
        win_v = win_d.rearrange("(k p) c -> p k c", p=128)

        def load_mixer_weights(gate):
            k.dma("pool", k.dsem(), w_in[:, :, 0:512], win_v[:, :, 0:512], reads=[gate[0]], writes=[b_winP])
            k.dma("pool", k.dsem(), poolw[:], pw_d.rearrange("g c d -> c g d"), reads=[gate[0]], writes=[b_poolw])
            for hf in range(2):
                for (c0, bb) in ((1024, b_winC), (1536, b_winH), (512, b_winB)):
                    a = c0 + hf * 256
                    k.dma("pool", k.dsem(), w_in[:, :, a:a + 256], win_v[:, :, a:a + 256],
                          reads=[gate[1 + hf]], writes=[bb[hf]])
            first_half = [b_winP, b_winC[0], b_winH[0], b_winB[0]]
            for s in range(4, NSUB):
                k.dma("sp", x_sems[s], x1[:, s, :], x_d[s * 128:(s + 1) * 128, :],
                      reads=(first_half if s < 8 else b_win_all), writes=[b_x[s]])
        wos = k.dsem()
        wout_v = wout_d.rearrange("(k p) d -> p k d", p=128)

        def load_wout():
            for hh in range(2):
                k.dma("pool", wos, w_out[:, :, hh * 512:(hh + 1) * 512], wout_v[:, :, hh * 512:(hh + 1) * 512],
                      reads=[b_x[NSUB - 1]], writes=[b_wout[hh]])

        stat_i = [0]

        class NormJob:
            pass

        def nj_new(xsrc_ap, rows, b_xsrc, gain_ap, b_g, out_ap, b_out, junk_ap, b_jk):
            j = NormJob()
            j.x, j.rows, j.bx, j.g, j.bg = xsrc_ap, rows, b_xsrc, gain_ap, b_g
            j.out, j.bout, j.junk, j.bjk = out_ap, b_out, junk_ap, b_jk
            j.i = stat_i[0]
            stat_i[0] += 1
            j.bs, j.bm, j.bi = Buf("ss"), Buf("ms"), Buf("inv")
            return j

        def nj_square(j):
            i, rows = j.i, j.rows
            k.op("act", lambda: ra.activation(out=j.junk, in_=j.x, func=AF.Square,
                                                     accum_out=ss[:rows, i:i + 1]),
                 reads=[j.bx], writes=[j.bs], scratch=[j.bjk])

        def nj_inv(j):
            i, rows = j.i, j.rows
            k.op("dve", lambda: rv.tensor_scalar(ms[:rows, i:i + 1], ss[:rows, i:i + 1], 1.0 / D, EPS,
                                                        op0=ALU.mult, op1=ALU.add),
                 reads=[j.bs], writes=[j.bm])
            k.op("pool", lambda: rp.tensor_tensor(out=inv[:rows, i:i + 1], in0=ms[:rows, i:i + 1],
                                                         in1=mhalf[:rows, 0:1], op=ALU.pow),
                 reads=[j.bm, b_mhalf], writes=[j.bi])

        def nj_scale(j, split=False):
            i, rows = j.i, j.rows
            if split:
                k.op("act", lambda: ra.mul(out=j.out, in_=j.x, mul=inv[:rows, i:i + 1]),
                     reads=[j.bx, j.bi], writes=[j.bout])
                k.op("pool", lambda: rp.tensor_tensor(out=j.out, in0=j.out, in1=j.g, op=ALU.mult),
                     reads=[j.bout, j.bg], writes=[j.bout])
                return
            k.op("dve", lambda: rv.scalar_tensor_tensor(out=j.out, in0=j.x, scalar=inv[:rows, i:i + 1],
                                                               in1=j.g, op0=ALU.mult, op1=ALU.mult),
                 reads=[j.bx, j.bi, j.bg], writes=[j.bout])

        def norm_stage_b(rows, slot, col0, b_dst, copy_eng="act"):
            def tr():
                ins = None
                for kk in range(KD):
                    ins = rt.transpose(out=psT[:, kk, :rows], in_=hn[slot][:rows, kk * 128:(kk + 1) * 128],
                                              identity=ident[:rows, :rows])
                return ins
            k.op("pe", tr, reads=[b_hn[slot], b_ident], writes=[b_psT])
            dst = hnT[:, :, col0:col0 + rows]
            if copy_eng == "dve":
                k.op("dve", lambda: rv.tensor_copy(dst, psT[:, :, :rows]), reads=[b_psT], writes=[b_dst])
            else:
                k.op("act", lambda: ra.copy(out=dst, in_=psT[:, :, :rows]), reads=[b_psT], writes=[b_dst])

        p1_items = [("h", H)] + [(s, 128) for s in range(NSUB)]

        p1_jobs = {}
        for q in range(5):
            hn.append(at(RB + q * 2048, [128, D], BF16, "hns"))
            b_hn.append(Buf("hns%d" % q))

        def p1_slot(idx):
            return 2 + idx if idx < 5 else idx % 2

        def p1_job(idx):
            if idx not in p1_jobs:
                s_, rows = p1_items[idx]
                sl = p1_slot(idx)
                if s_ == "h":
                    p1_jobs[idx] = nj_new(xh[:rows, :], rows, b_xh, gain[:rows, :], b_gain,
                                          hn[sl][:rows, :], b_hn[sl], hn[sl][:rows, :], b_hn[sl])
                else:
                    p1_jobs[idx] = nj_new(x1[:, s_, :], rows, b_x[s_], gain[:, :], b_gain,
                                          hn[sl][:, :], b_hn[sl], hn[sl][:, :], b_hn[sl])
            return p1_jobs[idx]

        def p1_sq(idx):
            nj_square(p1_job(idx))
            nj_inv(p1_job(idx))

        def p1_sc(idx):
            nj_scale(p1_job(idx))

        def p1_b(idx):
            s_, rows = p1_items[idx]
            ce = "dve" if idx in (2, 4) else "act"
            if s_ == "h":
                norm_stage_b(rows, p1_slot(idx), 0, b_hnTh, ce)
            else:
                norm_stage_b(rows, p1_slot(idx), H + s_ * 128, b_hnT[s_], ce)

        b_V = [Buf("V0"), Buf("V1")]
        b_SA, b_SB = Buf("SA"), Buf("SB")
        b_PL = [Buf("PL0"), Buf("PL1")]
        b_CS, b_TC, b_T16, b_U = Buf("CS"), Buf("TC"), Buf("T16"), Buf("U")
        b_Vc = [Buf("Vc%d" % i) for i in range(4)]
        b_Uc = [Buf("Uc%d" % i) for i in range(4)]
        p1_bufs = b_hn[0:2] + [b_gain, b_xh]
        p2_bufs = b_V + [b_SA, b_SB] + b_PL + [b_CS, b_TC, b_T16, b_U] + b_Vc + b_Uc

        rot = {}

        def next_bank(role_banks, key):
            i = rot.get(key, 0)
            rot[key] = i + 1
            return role_banks[i % len(role_banks)]

        def hn_reads(t):
            return [b_hnT[4 * t + j] for j in range(4)]

        def inproj_pairs(col0, ncols, tok0, ntok):
            return [(w_in[:, kk, col0:col0 + ncols], hnT[:, kk, tok0:tok0 + ntok]) for kk in range(KD)]

        M_banks = [0, 1, 2, 3, 4, 5, 6]
        pcount = [0]

        def pool_front(gi, t):
            q = pcount[0] % 2
            pcount[0] += 1
            w = WINS[gi]
            c0 = gi * 128
            if t == 0:
                bk = next_bank(M_banks, "m")
                k.mm(psb[bk][:, 0:H], inproj_pairs(c0, 128, 0, H), reads=[b_winP, b_hnTh], writes=[b_ps[bk]])
                k.op("act", lambda: ra.copy(out=V[q][:, 0:H], in_=psb[bk][:, 0:H]),
                     reads=[b_ps[bk]], writes=[b_V[q]])
            else:
                k.op("act", lambda: ra.copy(out=V[q][:, 0:H], in_=Vc[:, gi, :]),
                     reads=[b_Vc[gi]], writes=[b_V[q]])
            bk = next_bank(M_banks, "m")
            k.mm(psb[bk][:, :], inproj_pairs(c0, 128, H + t * TT, TT), reads=[b_winP] + hn_reads(t),
                 writes=[b_ps[bk]])
            k.op("act", lambda: ra.copy(out=V[q][:, H:H + TT], in_=psb[bk][:, :]),
                 reads=[b_ps[bk]], writes=[b_V[q]])
            if t < NTILE - 1:
                k.op("act", lambda: ra.copy(out=Vc[:, gi, :], in_=V[q][:, TT:TT + H]),
                     reads=[b_V[q]], writes=[b_Vc[gi]])
            L = TT + H
            k.op("pool", lambda: rp.tensor_tensor(out=SA[:, 1:L], in0=V[q][:, 1:L], in1=V[q][:, 0:L - 1],
                                                         op=ALU.add), reads=[b_V[q]], writes=[b_SA])
            S, bS = SA, b_SA
            if w >= 4:
                k.op("pool", lambda: rp.tensor_tensor(out=SB[:, 3:L], in0=SA[:, 3:L], in1=SA[:, 1:L - 2],
                                                             op=ALU.add), reads=[b_SA], writes=[b_SB])
                S, bS = SB, b_SB
            if w >= 8:
                k.op("pool", lambda: rp.tensor_tensor(out=SA[:, 7:L], in0=SB[:, 7:L], in1=SB[:, 3:L - 4],
                                                             op=ALU.add), reads=[b_SB], writes=[b_SA])
                S, bS = SA, b_SA
            if w >= 16:
                k.op("pool", lambda: rp.tensor_tensor(out=SB[:, 15:L], in0=SA[:, 15:L], in1=SA[:, 7:L - 8],
                                                             op=ALU.add), reads=[b_SA], writes=[b_SB])
                S, bS = SB, b_SB
            k.op("dve", lambda: rv.scalar_tensor_tensor(out=PL[q][:, :], in0=S[:, H:L], scalar=1.0 / w,
                                                               in1=V[q][:, H:L], op0=ALU.mult, op1=ALU.subtract),
                 reads=[bS, b_V[q]], writes=[b_PL[q]])
            if t == 0:
                k.op("dve", lambda: rv.tensor_tensor(out=T16[:, :], in0=S[:, H:2 * H],
                                                            in1=invc[:, gi * H:(gi + 1) * H], op=ALU.mult),
                     reads=[bS, b_invc], writes=[b_T16])
                k.op("dve", lambda: rv.tensor_tensor(out=PL[q][:, 0:H], in0=T16[:, :], in1=V[q][:, H:2 * H],
                                                            op=ALU.subtract),
                     reads=[b_T16, b_V[q], b_PL[q]], writes=[b_PL[q]])

            def back():
                bk2 = next_bank(M_banks, "m")
                k.mm(psb[bk2][:, :], [(poolw[:, gi, :], PL[q][:, :])], reads=[b_poolw, b_PL[q]], writes=[b_ps[bk2]])
                k.op("act", lambda: ra.mul(out=ymix[:, gi, t * TT:(t + 1) * TT], in_=psb[bk2][:, :],
                                                  mul=pscale[:, gi:gi + 1]),
                     reads=[b_ps[bk2], b_pscale], writes=[b_ymix[t][gi]])
            return back

        def conv_front(j, t):
            hf = j // 2
            cB, cC, ch = 512 + j * 128, 1024 + j * 128, 1536 + j * 128
            if t == 0:
                bkc = next_bank(M_banks, "m")
                k.mm(psb[bkc][:, 0:2], inproj_pairs(cC, 128, H - 2, 2), reads=[b_winC[hf], b_hnTh],
                     writes=[b_ps[bkc]])
                bkh = next_bank(M_banks, "m")
                k.mm(psb[bkh][:, 0:2], inproj_pairs(ch, 128, H - 2, 2), reads=[b_winH[hf], b_hnTh],
                     writes=[b_ps[bkh]])
                k.op("act", lambda: ra.copy(out=CS[:, 0:2], in_=psb[bkc][:, 0:2]),
                     reads=[b_ps[bkc]], writes=[b_CS])
                k.op("dve", lambda: rv.tensor_tensor(out=U[:, 0:2], in0=CS[:, 0:2], in1=psb[bkh][:, 0:2],
                                                            op=ALU.mult),
                     reads=[b_CS, b_ps[bkh]], writes=[b_U])
            else:
                k.op("act", lambda: ra.copy(out=U[:, 0:2], in_=Uc[:, j, :]),
                     reads=[b_Uc[j]], writes=[b_U])
            bkc = next_bank(M_banks, "m")
            k.mm(psb[bkc][:, :], inproj_pairs(cC, 128, H + t * TT, TT), reads=[b_winC[hf]] + hn_reads(t),
                 writes=[b_ps[bkc]])
            bkh = next_bank(M_banks, "m")
            k.mm(psb[bkh][:, :], inproj_pairs(ch, 128, H + t * TT, TT), reads=[b_winH[hf]] + hn_reads(t),
                 writes=[b_ps[bkh]])
            bkb = next_bank(M_banks, "m")
            k.mm(psb[bkb][:, :], inproj_pairs(cB, 128, H + t * TT, TT), reads=[b_winB[hf]] + hn_reads(t),
                 writes=[b_ps[bkb]])
            k.op("act", lambda: ra.copy(out=CS[:, :], in_=psb[bkc][:, :]), reads=[b_ps[bkc]], writes=[b_CS])
            k.op("dve", lambda: rv.tensor_tensor(out=U[:, 2:TT + 2], in0=CS[:, :], in1=psb[bkh][:, :],
                                                        op=ALU.mult),
                 reads=[b_CS, b_ps[bkh], b_U], writes=[b_U])
            k.op("act", lambda: ra.mul(out=TC[:, :], in_=U[:, 2:TT + 2], mul=convw[:, j * 3 + 2:j * 3 + 3]),
                 reads=[b_U, b_convw], writes=[b_TC])
            if t < NTILE - 1:
                k.op("act", lambda: ra.copy(out=Uc[:, j, :], in_=U[:, TT:TT + 2]),
                     reads=[b_U], writes=[b_Uc[j]])
            k.op("dve", lambda: rv.scalar_tensor_tensor(out=TC[:, :], in0=U[:, 1:TT + 1],
                                                               scalar=convw[:, j * 3 + 1:j * 3 + 2], in1=TC[:, :],
                                                               op0=ALU.mult, op1=ALU.add),
                 reads=[b_U, b_convw, b_TC], writes=[b_TC])
            k.op("dve", lambda: rv.scalar_tensor_tensor(out=TC[:, :], in0=U[:, 0:TT],
                                                               scalar=convw[:, j * 3:j * 3 + 1], in1=TC[:, :],
                                                               op0=ALU.mult, op1=ALU.add),
                 reads=[b_U, b_convw, b_TC], writes=[b_TC])
            k.op("dve", lambda: rv.tensor_tensor(out=ymix[:, 4 + j, t * TT:(t + 1) * TT], in0=TC[:, :],
                                                        in1=psb[bkb][:, :], op=ALU.mult),
                 reads=[b_TC, b_ps[bkb]], writes=[b_ymix[t][4 + j]])
            return None

        for idx in range(5):
            p1_sq(idx)
        load_mixer_weights([p1_job(GATE_JOBS[0]).bm, p1_job(GATE_JOBS[1]).bm, p1_job(GATE_JOBS[2]).bm])
        for idx in range(5):
            p1_sc(idx)
        for idx in range(5):
            p1_b(idx)
        ev_hns = retire(*b_hn[2:7])
        for tt_ in range(NTILE):
            for cc in range(3):
                inherit(b_ymix[tt_][cc], ev_hns)
        for t in range(NTILE):
            if t == 0:
                order = [("p", 0), ("p", 1), ("p", 2), ("p", 3), ("c", 0), ("c", 1), ("c", 2), ("c", 3)]
            else:
                order = [("p", 0), ("c", 0), ("p", 1), ("c", 1), ("p", 2), ("c", 2), ("p", 3), ("c", 3)]
            pending = []
            nb = 1 + 4 * (t + 1)
            if t + 1 < NTILE:
                p1_sq(nb)
            for i, (kind, a) in enumerate(order):
                back = pool_front(a, t) if kind == "p" else conv_front(a, t)
                if kind == "p" and pending:
                    pending.pop(0)()
                if back is not None:
                    pending.append(back)
                if t + 1 < NTILE:
                    kq = i // 2
                    if i % 2 == 0:
                        p1_sc(nb + kq)
                        if i == 0:
                            p1_sq(nb + 1)
                    else:
                        p1_b(nb + kq)
                        if 1 <= kq <= 2:
                            p1_sq(nb + kq + 1)
            while pending:
                pending.pop(0)()
            if t == 0:
                load_wout()

        ffn_sems = [k.dsem(), k.dsem()]
        b_wg = [Buf("wg0"), Buf("wg1")]
        b_wu = [Buf("wu0"), Buf("wu1")]
        b_wd = [Buf("wd0"), Buf("wd1")]
        wg_v = wg_d.rearrange("(k p) f -> p k f", p=128)
        wu_v = wu_d.rearrange("(k p) f -> p k f", p=128)
        wd_v = wd_d.rearrange("(c p) d -> p c d", p=128)

        def load_group(g):
            q = g % 2
            c0, ncn = GROUPS[g]
            f0, nf = c0 * 128, ncn * 128
            ds = ffn_sems[q]
            k.dma("pool", ds, wg_s[q][:, :, 0:nf], wg_v[:, :, f0:f0 + nf], writes=[b_wg[q]])
            k.dma("pool", ds, wu_s[q][:, :, 0:nf], wu_v[:, :, f0:f0 + nf], writes=[b_wu[q]])
            k.dma("pool", ds, wd_s[q][:, 0:ncn, :], wd_v[:, c0:c0 + ncn, :], writes=[b_wd[q]])

        ev_ra = retire(*b_win_all)
        for b in (b_wg[0], b_wu[0], b_wd[0]):
            inherit(b, ev_ra)
        load_group(0)

        k.dma("sp", gs, gain[:], g2_d[:], writes=[b_gain])
        O_banks = [0, 1, 2, 3, 4, 5]
        rot["o"] = 0

        def p3_front(s, halves):
            t, s4 = s // 4, s % 4
            for hh in halves:
                bk = next_bank(O_banks, "o")
                pairs = [(ymix[:, kk, s * 128:(s + 1) * 128], w_out[:, kk, hh * 512:(hh + 1) * 512])
                         for kk in range(KD)]
                k.mm(psb[bk][:, :], pairs, reads=[b_wout[hh]] + b_ymix[t], writes=[b_ps[bk]])
                k.op("dve", lambda: rv.tensor_tensor(out=x1[:, s, hh * 512:(hh + 1) * 512],
                                                            in0=x1[:, s, hh * 512:(hh + 1) * 512],
                                                            in1=psb[bk][:, :], op=ALU.add),
                     reads=[b_x[s], b_ps[bk]], writes=[b_x[s]])

        junk3 = at(RA + 24576, [128, D], BF16, "junk3")
        b_junk3 = Buf("junk3")
        inherit(b_junk3, ev_ra)
        for q in range(2):
            hn.append(at(RA + 26624 + q * 2048, [128, D], BF16, "hnx"))
            bq = Buf("hn%d" % (2 + q))
            inherit(bq, ev_ra)
            b_hn.append(bq)
        p3_slots = [0, 1, len(hn) - 2, len(hn) - 1]
        p3_jobs = [nj_new(x1[:, s, :], 128, b_x[s], gain[:, :], b_gain, hn[p3_slots[s % 4]][:, :],
                          b_hn[p3_slots[s % 4]], junk3[:, :], b_junk3) for s in range(NSUB)]
        for i in range(NSUB + 3):
            if i < NSUB:
                p3_front(i, (0,))
                p3_front(i, (1,))
                nj_square(p3_jobs[i])
            if 0 <= i - 1 < NSUB:
                nj_inv(p3_jobs[i - 1])
            if 0 <= i - 2 < NSUB:
                nj_scale(p3_jobs[i - 2])
            if 0 <= i - 3 < NSUB:
                norm_stage_b(128, p3_slots[(i - 3) % 4], H + (i - 3) * 128, b_hnT[i - 3])

        ev_rb = retire(*[b for row in b_ymix for b in row])
        for b in (b_wg[1], b_wu[1], b_wd[1]):
            inherit(b, ev_rb)
        load_group(1)

        ev_p3 = retire(*(p1_bufs + p2_bufs))
        b_hT = [Buf("hT0"), Buf("hT1")]
        b_SG = [Buf("SG0"), Buf("SG1")]
        b_ST = [Buf("ST%d" % q) for q in range(NST)]
        b_gf = Buf("gf")
        for b in b_hT + b_SG + b_ST + [b_gf]:
            inherit(b, ev_p3)
        for b in b_ST[3:]:
            inherit(b, ev_rb)
        gfs = k.dsem()
        k.dma("sp", gfs, gfb[:], gf_d[:], writes=[b_gf])
        st_sems = [k.dsem(), k.dsem()]

        G_banks = [0, 1]
        U_banks = [2, 3]
        D_banks = [4, 5, 6, 7]
        ffn_items = [(g, t) for g in range(len(GROUPS)) for t in range(NTILE)]

        def ffn_gu(n, c):
            g, t = ffn_items[n]
            q, hq = g % 2, n % 2
            bg = next_bank(G_banks, "g")
            bu = next_bank(U_banks, "u")
            tok0 = H + t * TT
            k.mm(psb[bg][:, :], [(wg_s[q][:, kk, c * 128:(c + 1) * 128], hnT[:, kk, tok0:tok0 + TT]) for kk in range(KD)],
                 reads=[b_wg[q]] + hn_reads(t), writes=[b_ps[bg]])
            k.mm(psb[bu][:, :], [(wu_s[q][:, kk, c * 128:(c + 1) * 128], hnT[:, kk, tok0:tok0 + TT]) for kk in range(KD)],
                 reads=[b_wu[q]] + hn_reads(t), writes=[b_ps[bu]])
            sq = next_bank([0, 1], "sg")
            k.op("act", lambda: ra.activation(out=SG[sq][:, :], in_=psb[bg][:, :], func=AF.Tanh, scale=0.5),
                 reads=[b_ps[bg]], writes=[b_SG[sq]])
            k.op("dve", lambda: rv.scalar_tensor_tensor(out=SG[sq][:, :], in0=SG[sq][:, :], scalar=1.0,
                                                        in1=psb[bg][:, :], op0=ALU.add, op1=ALU.mult),
                 reads=[b_SG[sq], b_ps[bg]], writes=[b_SG[sq]])
            k.op("dve", lambda: rv.scalar_tensor_tensor(out=hT[hq][:, c, :], in0=SG[sq][:, :], scalar=0.5,
                                                        in1=psb[bu][:, :], op0=ALU.mult, op1=ALU.mult),
                 reads=[b_SG[sq], b_ps[bu], b_hT[hq]], writes=[b_hT[hq]])

        def ffn_down(n, dsel):
            g, t = ffn_items[n]
            q, hq = g % 2, n % 2
            ncn = GROUPS[g][1]
            for s4 in (dsel // 2,):
                s = 4 * t + s4
                for hh in (dsel % 2,):
                    bk = next_bank(D_banks, "d")
                    pairs = [(hT[hq][:, c, s4 * 128:(s4 + 1) * 128], wd_s[q][:, c, hh * 512:(hh + 1) * 512])
                             for c in range(ncn)]
                    k.mm(psb[bk][:, :], pairs, reads=[b_hT[hq], b_wd[q]], writes=[b_ps[bk]])
                    k.op("dve", lambda: rv.tensor_tensor(out=x1[:, s, hh * 512:(hh + 1) * 512],
                                                                in0=x1[:, s, hh * 512:(hh + 1) * 512],
                                                                in1=psb[bk][:, :], op=ALU.add),
                         reads=[b_x[s], b_ps[bk]], writes=[b_x[s]])
                if g == len(GROUPS) - 1 and dsel % 2 == 1:
                    final_step(s)

        fin_jobs = [nj_new(x1[:, s, :], 128, b_x[s], gfb[:, :], b_gf, ST[s % NST][:, :], b_ST[s % NST],
                           junk3[:, :], b_junk3) for s in range(NSUB)]

        def final_step(i):
            if i < NSUB:
                nj_square(fin_jobs[i])
            if 0 <= i - 1 < NSUB:
                nj_inv(fin_jobs[i - 1])
            if 0 <= i - 2 < NSUB:
                s2 = i - 2
                nj_scale(fin_jobs[s2], split=FINAL_SPLIT)
                k.dma("sp", None, y_d[s2 * 128:(s2 + 1) * 128, :], ST[s2 % NST][:, :], reads=[b_ST[s2 % NST]],
                      store=True)

        nitems = len(ffn_items)
        for c in range(GROUPS[0][1]):
            ffn_gu(0, c)
        for n in range(nitems):
            g, t = ffn_items[n]
            nxt = n + 1
            gn = GROUPS[ffn_items[nxt][0]][1] if nxt < nitems else 0
            if gn:
                ffn_gu(nxt, 0)
            for dsel in range(8):
                ffn_down(n, dsel)
                c = (dsel + 1) // 2
                if dsel % 2 == 1 and 1 <= c < gn:
                    ffn_gu(nxt, c)
            if t == NTILE - 1 and g + 2 < len(GROUPS):
                load_group(g + 2)

        final_step(NSUB)
        final_step(NSUB + 1)
        k.schedule()
        k.emit()
    return nc


def _prep_inputs(inputs):
    f = lambda a: np.ascontiguousarray(np.asarray(a, dtype=np.float32))
    x = f(inputs["x"])
    rep = lambda g: np.ascontiguousarray(np.broadcast_to(f(g)[None, :], (128, D)))
    shared = {
        "g1": rep(inputs["norm1_g"]),
        "g2": rep(inputs["norm2_g"]),
        "gf": rep(inputs["normf_g"]),
        "w_in": f(inputs["w_in"]),
        "pool_w": f(inputs["pool_w"]),
        "pscale": np.ascontiguousarray(f(inputs["pool_scale"]).reshape(4, 128).T),
        "convw": np.ascontiguousarray(f(inputs["conv_w"]).reshape(3, 4, 128).transpose(2, 1, 0).reshape(128, 12)),
        "w_out": f(inputs["w_out"]),
        "w_gate": f(inputs["w_gate"]),
        "w_up": f(inputs["w_up"]),
        "w_down": f(inputs["w_down"]),
        "ident": np.eye(128, dtype=np.float32),
    }
    ic_start = np.zeros((4, H), np.float32)
    ic_mid = np.zeros((4, H), np.float32)
    for gi, w in enumerate(WINS):
        for t in range(H):
            ic_start[gi, t] = 1.0 / min(t + 1, w)
            ic_mid[gi, t] = 1.0 / w
    in_maps = []
    for c in range(N_CORES):
        b, qq = divmod(c, 4)
        t0 = qq * NT
        m = dict(shared)
        m["x"] = np.ascontiguousarray(x[b, t0:t0 + NT, :])
        if qq == 0:
            m["xh"] = np.zeros((H, D), np.float32)
            ic = ic_start
        else:
            m["xh"] = np.ascontiguousarray(x[b, t0 - H:t0, :])
            ic = ic_mid
        m["invcnt"] = np.ascontiguousarray(np.broadcast_to(ic.reshape(1, 4 * H), (128, 4 * H)))
        in_maps.append(m)
    return in_maps


def kernel(**inputs):
    in_maps = _prep_inputs(inputs)
    nc = build_program()
    res = run_bass_kernel_spmd(nc, in_maps, core_ids=list(range(N_CORES)))
    out = np.empty((2, 4 * NT, D), np.float32)
    for c in range(N_CORES):
        b, qq = divmod(c, 4)
        out[b, qq * NT:(qq + 1) * NT, :] = np.asarray(res.results[c]["y"], dtype=np.float32)
    return out
```
